# Optimizing a Trainium2 kernel written in Bass

```python
import math
import jax, jax.numpy as jnp
from jax import lax
import numpy as np

D_MODEL = 2048
BATCH = 4
SEQ = 2048
DEPTH = 1
DEC_BATCH = 128
DEC_SEQ = 4
PAST_LEN = 16384
PAGE_SIZE = 128

SSM_WIDTH = D_MODEL // 2
SSM_HEADDIM = 64
SSM_HEADS = SSM_WIDTH // SSM_HEADDIM
SSM_GROUPS = 2
SSM_REP = SSM_HEADS // SSM_GROUPS
D_STATE = 128
CONV_WIDTH = 4
CONV_DIM = SSM_WIDTH + 2 * SSM_GROUPS * D_STATE
SSD_CHUNK = 128
ATT_WIDTH = D_MODEL - SSM_WIDTH
HEAD_DIM = 64
ATT_HEADS = ATT_WIDTH // HEAD_DIM
KV_HEADS = 2
KV_REP = ATT_HEADS // KV_HEADS
WINDOW = 128
ROT_DIM = HEAD_DIM // 4
ROPE_THETA = 500000.0
MIX_WIDTH = SSM_WIDTH + ATT_WIDTH
IN_DIM = SSM_WIDTH + CONV_DIM + SSM_HEADS + ATT_WIDTH + 2 * KV_HEADS * HEAD_DIM
N_MEM = 256
X_HEADS = 4
X_HEAD_DIM = 128
X_WIDTH = X_HEADS * X_HEAD_DIM
D_FF = -(-8 * D_MODEL // (3 * 256)) * 256
EPS = 1e-5

kernel_name = 'hymba_ssd_swa_memxattn_step'


def rmsnorm(x, w):
    xf = x.astype(jnp.float32)
    y = xf * lax.rsqrt(jnp.mean(xf * xf, axis=-1, keepdims=True) + EPS)
    return (y * w.astype(jnp.float32)).astype(x.dtype)


def rope(x, pos):
    half = ROT_DIM // 2
    inv = ROPE_THETA ** (-jnp.arange(half, dtype=jnp.float32) * (2.0 / ROT_DIM))
    ang = pos.astype(jnp.float32)[:, None] * inv[None, :]
    cos = jnp.cos(ang)[None, :, None, :]
    sin = jnp.sin(ang)[None, :, None, :]
    xr = x[..., :ROT_DIM].astype(jnp.float32)
    x1, x2 = xr[..., :half], xr[..., half:]
    rot = jnp.concatenate([x1 * cos - x2 * sin, x2 * cos + x1 * sin], axis=-1).astype(x.dtype)
    return jnp.concatenate([rot, x[..., ROT_DIM:]], axis=-1)


def causal_conv(xbc, buf, w, bias):
    T = xbc.shape[1]
    xp = jnp.concatenate([buf.astype(xbc.dtype), xbc], axis=1)
    out = bias
    for k in range(CONV_WIDTH):
        out = out + xp[:, k:k + T] * w[k]
    return jax.nn.silu(out), xp[:, -(CONV_WIDTH - 1):]


def ssd_scan(x, dt, a, bmat, cmat, h0):
    b, T, G, R, P = x.shape
    N = bmat.shape[-1]
    cl = SSD_CHUNK if T % SSD_CHUNK == 0 else T
    nc = T // cl
    f32 = jnp.float32
    xr = x.astype(f32).reshape(b, nc, cl, G, R, P)
    dtr = dt.astype(f32).reshape(b, nc, cl, G, R)
    br = bmat.astype(f32).reshape(b, nc, cl, G, N)
    cr = cmat.astype(f32).reshape(b, nc, cl, G, N)
    cs = jnp.cumsum(dtr * a.astype(f32), axis=2)
    causal = jnp.tril(jnp.ones((cl, cl), bool))[None, None, :, :, None, None]
    seg = cs[:, :, :, None] - cs[:, :, None, :]
    decay = jnp.exp(jnp.where(causal, seg, -jnp.inf))
    cb = jnp.einsum('bctgn,bcsgn->bctsg', cr, br)
    wts = cb[..., None] * decay * dtr[:, :, None]
    y_diag = jnp.einsum('bctsgr,bcsgrp->bctgrp', wts, xr)
    to_end = jnp.exp(cs[:, :, -1:] - cs) * dtr
    chunk_states = jnp.einsum('bclgn,bclgrp->bcgrpn', br, xr * to_end[..., None])
    chunk_decay = jnp.exp(cs[:, :, -1])

    def step(h, inp):
        st, dec = inp
        return h * dec[..., None, None] + st, h

    h_last, h_starts = lax.scan(step, h0.astype(f32),
                                (jnp.moveaxis(chunk_states, 1, 0), jnp.moveaxis(chunk_decay, 1, 0)))
    h_starts = jnp.moveaxis(h_starts, 0, 1)
    y_off = jnp.einsum('bclgn,bcgrpn->bclgrp', cr, h_starts) * jnp.exp(cs)[..., None]
    y = (y_diag + y_off).reshape(b, T, G, R, P)
    return y, h_last


def sink_softmax(s, mask, sink):
    s = jnp.where(mask, s, -jnp.inf)
    m = jnp.maximum(jnp.max(s, axis=-1, keepdims=True), sink)
    e = jnp.exp(s - m)
    return e / (jnp.sum(e, axis=-1, keepdims=True) + jnp.exp(sink - m))


def swa_banded(q, k, v, sinks):
    b, T = q.shape[0], q.shape[1]
    nb = T // WINDOW
    qb = q.reshape(b, nb, WINDOW, KV_HEADS, KV_REP, HEAD_DIM)
    pad = jnp.zeros((b, WINDOW, KV_HEADS, HEAD_DIM), k.dtype)
    kp = jnp.concatenate([pad, k], axis=1).reshape(b, nb + 1, WINDOW, KV_HEADS, HEAD_DIM)
    vp = jnp.concatenate([pad, v], axis=1).reshape(b, nb + 1, WINDOW, KV_HEADS, HEAD_DIM)
    kb = jnp.concatenate([kp[:, :-1], kp[:, 1:]], axis=2)
    vb = jnp.concatenate([vp[:, :-1], vp[:, 1:]], axis=2)
    s = jnp.einsum('bnqgrd,bnkgd->bngrqk', qb, kb).astype(jnp.float32) * HEAD_DIM ** -0.5
    qi = jnp.arange(WINDOW)[:, None]
    kj = jnp.arange(2 * WINDOW)[None, :]
    rel = qi + WINDOW - kj
    blk = jnp.arange(nb)[:, None, None]
    mask = (rel >= 0) & (rel < WINDOW) & (blk * WINDOW - WINDOW + kj >= 0)
    p = sink_softmax(s, mask[None, :, None, None], sinks)
    o = jnp.einsum('bngrqk,bnkgd->bnqgrd', p.astype(v.dtype), vb)
    return o.reshape(b, T, ATT_HEADS, HEAD_DIM)


def swa_cached(q, kcat, vcat, q_pos, k_pos, sinks):
    b, T = q.shape[0], q.shape[1]
    qg = q.reshape(b, T, KV_HEADS, KV_REP, HEAD_DIM)
    s = jnp.einsum('btgrd,bkgd->bgrtk', qg, kcat).astype(jnp.float32) * HEAD_DIM ** -0.5
    rel = q_pos[:, None] - k_pos[None, :]
    mask = (rel >= 0) & (rel < WINDOW)
    p = sink_softmax(s, mask, sinks)
    o = jnp.einsum('bgrtk,bkgd->btgrd', p.astype(vcat.dtype), vcat)
    return o.reshape(b, T, ATT_HEADS, HEAD_DIM)


def mixer(h, pos, ssm0, conv0, kbuf, vbuf, lp):
    b, T, _ = h.shape
    s1 = SSM_WIDTH
    s2 = s1 + CONV_DIM
    s3 = s2 + SSM_HEADS
    s4 = s3 + ATT_WIDTH
    s5 = s4 + KV_HEADS * HEAD_DIM
    proj = h @ lp['w_in']
    z, xbc, dt_raw, q, k, v = jnp.split(proj, [s1, s2, s3, s4, s5], axis=-1)
    xbc, conv_new = causal_conv(xbc, conv0, lp['conv_w'], lp['conv_b'])
    xs, bm, cm = jnp.split(xbc, [SSM_WIDTH, SSM_WIDTH + SSM_GROUPS * D_STATE], axis=-1)
    xs = xs.reshape(b, T, SSM_GROUPS, SSM_REP, SSM_HEADDIM)
    dt = jax.nn.softplus((dt_raw + lp['dt_bias']).astype(jnp.float32)).reshape(b, T, SSM_GROUPS, SSM_REP)
    a = -jnp.exp(lp['a_log'].astype(jnp.float32)).reshape(SSM_GROUPS, SSM_REP)
    y, ssm_new = ssd_scan(xs, dt, a,
                          bm.reshape(b, T, SSM_GROUPS, D_STATE), cm.reshape(b, T, SSM_GROUPS, D_STATE),
                          ssm0.reshape(b, SSM_GROUPS, SSM_REP, SSM_HEADDIM, D_STATE))
    y = y + xs.astype(jnp.float32) * lp['d_skip'].astype(jnp.float32).reshape(SSM_GROUPS, SSM_REP)[:, :, None]
    y = y.astype(h.dtype).reshape(b, T, SSM_WIDTH)
    g = (y * jax.nn.silu(z)).reshape(b, T, SSM_GROUPS, SSM_WIDTH // SSM_GROUPS)
    y = rmsnorm(g, lp['gate_norm'].reshape(SSM_GROUPS, -1)).reshape(b, T, SSM_WIDTH)
    q = rope(q.reshape(b, T, ATT_HEADS, HEAD_DIM), pos)
    k = rope(k.reshape(b, T, KV_HEADS, HEAD_DIM), pos)
    v = v.reshape(b, T, KV_HEADS, HEAD_DIM)
    sinks = lp['sinks'].astype(jnp.float32).reshape(KV_HEADS, KV_REP, 1, 1)
    if kbuf is None:
        att = swa_banded(q, k, v, sinks)
        k_new, v_new = k[:, -WINDOW:], v[:, -WINDOW:]
    else:
        nbuf = kbuf.shape[1]
        kcat = jnp.concatenate([kbuf.astype(k.dtype), k], axis=1)
        vcat = jnp.concatenate([vbuf.astype(v.dtype), v], axis=1)
        k_pos = pos[0] - nbuf + jnp.arange(nbuf + T)
        att = swa_cached(q, kcat, vcat, pos, k_pos, sinks)
        k_new, v_new = kcat[:, -nbuf:], vcat[:, -nbuf:]
    out = jnp.concatenate([y, att.reshape(b, T, ATT_WIDTH)], axis=-1) @ lp['w_out']
    ssm_new = ssm_new.reshape(b, SSM_HEADS, SSM_HEADDIM, D_STATE).astype(ssm0.dtype)
    return out, ssm_new, conv_new, k_new, v_new


def memory_kv(mem, lp):
    m = rmsnorm(mem, lp['norm_mem'])
    b, M, _ = m.shape
    mk = (m @ lp['w_xk']).reshape(b, M, X_HEADS, X_HEAD_DIM)
    mv = (m @ lp['w_xv']).reshape(b, M, X_HEADS, X_HEAD_DIM)
    return mk, mv


def cross_attn(h, mk, mv, lp):
    b, T, _ = h.shape
    q = (h @ lp['w_xq']).reshape(b, T, X_HEADS, X_HEAD_DIM)
    s = jnp.einsum('bthd,bmhd->bhtm', q, mk.astype(q.dtype)).astype(jnp.float32) * X_HEAD_DIM ** -0.5
    p = jax.nn.softmax(s, axis=-1).astype(h.dtype)
    o = jnp.einsum('bhtm,bmhd->bthd', p, mv.astype(h.dtype)).reshape(b, T, X_WIDTH)
    return o @ lp['w_xo']


def ffn(h, lp):
    return (jax.nn.silu(h @ lp['w_gate']) * (h @ lp['w_up'])) @ lp['w_down']


def layer(x, pos, ssm0, conv0, kbuf, vbuf, mk, mv, lp):
    mix, ssm_new, conv_new, k_new, v_new = mixer(rmsnorm(x, lp['norm_mix']), pos, ssm0, conv0, kbuf, vbuf, lp)
    x = x + mix
    x = x + cross_attn(rmsnorm(x, lp['norm_x']), mk, mv, lp)
    x = x + ffn(rmsnorm(x, lp['norm_ffn']), lp)
    return x, ssm_new, conv_new, k_new, v_new


def setup_inputs(seed: int = 0) -> dict:
    key = jax.random.key(seed)
    ks = jax.random.split(key, 30)
    L = DEPTH
    n_buf = min(WINDOW, PAST_LEN)

    def nrm(k, shape, scale):
        return jax.random.normal(k, shape, jnp.float32) * scale

    def gain(k, shape):
        return 1.0 + 0.05 * jax.random.normal(k, shape, jnp.float32)

    dt0 = jnp.exp(jax.random.uniform(ks[13], (L, SSM_HEADS), jnp.float32,
                                     minval=math.log(1e-3), maxval=math.log(1e-1)))
    dt_bias = dt0 + jnp.log(-jnp.expm1(-dt0))
    a_log = jnp.log(jax.random.uniform(ks[14], (L, SSM_HEADS), jnp.float32, minval=1.0, maxval=16.0))
    return {
        'x_prompt': nrm(ks[0], (BATCH, SEQ, D_MODEL), 1.0),
        'x_sample': nrm(ks[1], (DEC_BATCH, DEC_SEQ, D_MODEL), 1.0),
        'mem_prompt': nrm(ks[2], (BATCH, N_MEM, D_MODEL), 1.0),
        'state_ssm': nrm(ks[3], (L, DEC_BATCH, SSM_HEADS, SSM_HEADDIM, D_STATE), 0.5),
        'state_conv': nrm(ks[4], (L, DEC_BATCH, CONV_WIDTH - 1, CONV_DIM), 1.0),
        'cache_swa_k': nrm(ks[5], (L, DEC_BATCH, n_buf, KV_HEADS, HEAD_DIM), 1.0),
        'cache_swa_v': nrm(ks[6], (L, DEC_BATCH, n_buf, KV_HEADS, HEAD_DIM), 1.0),
        'cache_mem_k': nrm(ks[7], (L, DEC_BATCH, N_MEM, X_HEADS, X_HEAD_DIM), 1.0),
        'cache_mem_v': nrm(ks[8], (L, DEC_BATCH, N_MEM, X_HEADS, X_HEAD_DIM), 1.0),
        'norm_mix': gain(ks[9], (L, D_MODEL)),
        'w_in': nrm(ks[10], (L, D_MODEL, IN_DIM), D_MODEL ** -0.5),
        'conv_w': nrm(ks[11], (L, CONV_WIDTH, CONV_DIM), CONV_WIDTH ** -0.5),
        'conv_b': nrm(ks[12], (L, CONV_DIM), 0.02),
        'dt_bias': dt_bias,
        'a_log': a_log,
        'd_skip': gain(ks[15], (L, SSM_HEADS)),
        'gate_norm': gain(ks[16], (L, SSM_WIDTH)),
        'sinks': nrm(ks[17], (L, ATT_HEADS), 0.5),
        'w_out': nrm(ks[18], (L, MIX_WIDTH, D_MODEL), MIX_WIDTH ** -0.5),
        'norm_mem': gain(ks[19], (L, D_MODEL)),
        'norm_x': gain(ks[20], (L, D_MODEL)),
        'w_xq': nrm(ks[21], (L, D_MODEL, X_WIDTH), D_MODEL ** -0.5),
        'w_xk': nrm(ks[22], (L, D_MODEL, X_WIDTH), D_MODEL ** -0.5),
        'w_xv': nrm(ks[23], (L, D_MODEL, X_WIDTH), D_MODEL ** -0.5),
        'w_xo': nrm(ks[24], (L, X_WIDTH, D_MODEL), X_WIDTH ** -0.5),
        'norm_ffn': gain(ks[25], (L, D_MODEL)),
        'w_gate': nrm(ks[26], (L, D_MODEL, D_FF), D_MODEL ** -0.5),
        'w_up': nrm(ks[27], (L, D_MODEL, D_FF), D_MODEL ** -0.5),
        'w_down': nrm(ks[28], (L, D_FF, D_MODEL), D_FF ** -0.5),
        'norm_final': gain(ks[29], (D_MODEL,)),
    }


def reference(x_prompt, x_sample, mem_prompt, state_ssm, state_conv, cache_swa_k, cache_swa_v,
              cache_mem_k, cache_mem_v, norm_mix, w_in, conv_w, conv_b, dt_bias, a_log, d_skip,
              gate_norm, sinks, w_out, norm_mem, norm_x, w_xq, w_xk, w_xv, w_xo, norm_ffn,
              w_gate, w_up, w_down, norm_final):
    bp, tp = x_prompt.shape[0], x_prompt.shape[1]
    pos_p = jnp.arange(tp)
    pos_s = PAST_LEN + jnp.arange(x_sample.shape[1])
    xp, xs = x_prompt, x_sample
    p_ssm, p_conv, p_k, p_v, p_mk, p_mv = [], [], [], [], [], []
    s_ssm, s_conv, s_k, s_v = [], [], [], []
    for l in range(DEPTH):
        lp = {'norm_mix': norm_mix[l], 'w_in': w_in[l], 'conv_w': conv_w[l], 'conv_b': conv_b[l],
              'dt_bias': dt_bias[l], 'a_log': a_log[l], 'd_skip': d_skip[l], 'gate_norm': gate_norm[l],
              'sinks': sinks[l], 'w_out': w_out[l], 'norm_mem': norm_mem[l], 'norm_x': norm_x[l],
              'w_xq': w_xq[l], 'w_xk': w_xk[l], 'w_xv': w_xv[l], 'w_xo': w_xo[l],
              'norm_ffn': norm_ffn[l], 'w_gate': w_gate[l], 'w_up': w_up[l], 'w_down': w_down[l]}
        mk_p, mv_p = memory_kv(mem_prompt, lp)
        ssm0 = jnp.zeros((bp, SSM_HEADS, SSM_HEADDIM, D_STATE), x_prompt.dtype)
        conv0 = jnp.zeros((bp, CONV_WIDTH - 1, CONV_DIM), x_prompt.dtype)
        xp, a1, a2, a3, a4 = layer(xp, pos_p, ssm0, conv0, None, None, mk_p, mv_p, lp)
        p_ssm.append(a1); p_conv.append(a2); p_k.append(a3); p_v.append(a4)
        p_mk.append(mk_p); p_mv.append(mv_p)
        xs, b1, b2, b3, b4 = layer(xs, pos_s, state_ssm[l], state_conv[l], cache_swa_k[l], cache_swa_v[l],
                                   cache_mem_k[l], cache_mem_v[l], lp)
        s_ssm.append(b1); s_conv.append(b2); s_k.append(b3); s_v.append(b4)
    y_prompt = rmsnorm(xp, norm_final)
    y_sample = rmsnorm(xs, norm_final)
    return (y_prompt, y_sample,
            jnp.stack(p_ssm), jnp.stack(p_conv), jnp.stack(p_k), jnp.stack(p_v),
            jnp.stack(p_mk), jnp.stack(p_mv),
            jnp.stack(s_ssm), jnp.stack(s_conv), jnp.stack(s_k), jnp.stack(s_v))
```

```python
import numpy as np
import concourse.bass as bass
import concourse.mybir as mybir

F32 = mybir.dt.float32; BF16 = mybir.dt.bfloat16; I32 = mybir.dt.int32
AF = mybir.ActivationFunctionType
ALU = mybir.AluOpType
AX = mybir.AxisListType

def _prod(xs):
    r = 1
    for v in xs: r *= int(v)
    return r

def region(ap):
    t = ap.tensor
    e = mybir.dt.size(ap.dtype)
    space = str(ap.space)
    dims = [(int(s), int(c)) for (s, c) in ap.ap]
    off = int(ap.offset)
    if 'DRAM' in space.upper() or 'HBM' in space.upper() or not hasattr(t, 'shape') or 'DRam' in type(t).__name__:
        lo = off; hi = off
        for s, c in dims:
            if s >= 0: hi += s * (c - 1)
            else: lo += s * (c - 1)
        return (t.name, 0, 1, lo * e, (hi + 1) * e)
    rowbytes = _prod(list(t.shape)[1:]) * mybir.dt.size(t.dtype)
    row_e = rowbytes // e
    p0 = off // row_e
    f0 = off % row_e
    s0, c0 = dims[0]
    pstep = s0 // row_e if c0 > 1 else 1
    assert c0 == 1 or s0 % row_e == 0, (ap, "first dim must be partition dim")
    p1 = p0 + (c0 - 1) * max(pstep, 0) + 1
    lo = f0; hi = f0
    for s, c in dims[1:]:
        if s >= 0: hi += s * (c - 1)
        else: lo += s * (c - 1)
    b0 = lo * e; b1 = (hi + 1) * e
    if 'PSum' in type(t).__name__:
        b0 = (b0 // 2048) * 2048; b1 = ((b1 + 2047) // 2048) * 2048
        p0 = (p0 // 32) * 32; p1 = ((p1 + 31) // 32) * 32
    return (t.name, p0, p1, b0, b1)

class _Op:
    __slots__ = ('eng', 'fn', 'is_dma', 'pos', 'deps', 'signal', 'seq', 'dsem', 'dval', 'waits', 'is_pe_w', 'odeps', 'cost', 'idx', 'nbytes', 'tag')

class Sched:
    ENGS = ('sp', 'act', 'dve', 'pool', 'pe')
    def __init__(self, nc, es, n_dma_sems=40):
        self.nc = nc
        self.eng = {'sp': nc.sync, 'act': nc.scalar, 'dve': nc.vector, 'pool': nc.gpsimd, 'pe': nc.tensor}
        self.sem = {e: es.enter_context(nc.semaphore("S_" + e)) for e in self.ENGS}
        self.dsems = [es.enter_context(nc.semaphore("D%d" % i)) for i in range(2 * n_dma_sems)]
        self.n_pool = n_dma_sems
        self.dcount = {False: 0, True: 0}
        self.dhist = {False: [], True: []}
        self.ops = []
        self.recs = {}
        self.ndma = 0
        self.dma_ops = []
        self.cnt = {e: 0 for e in self.ENGS}

    def cutpoint(self, name):
        import os
        if os.environ.get('KCUT') == name:
            self.dead = True
            print("CUT at", name)
    def op(self, eng, fn, outs, ins, dma=False):
        if getattr(self, 'dead', False):
            return None
        o = _Op(); o.eng = eng; o.fn = fn; o.is_dma = dma; o.signal = False; o.seq = 0
        o.dsem = None; o.dval = 0; o.waits = None; o.tag = None
        idx = len(self.ops)
        o.pos = self.cnt[eng]; self.cnt[eng] += 1
        deps = set()
        rregs = [region(a) for a in ins if a is not None]
        wregs = [region(a) for a in outs if a is not None]
        for (k, p0, p1, b0, b1) in rregs:
            isps = (k == 'ps')
            for r in self.recs.get(k, ()):
                if (r[5] or (isps and self.ops[r[4]].eng != eng)) and r[0] < p1 and p0 < r[1] and r[2] < b1 and b0 < r[3]:
                    deps.add(r[4])
        for (k, p0, p1, b0, b1) in wregs:
            for r in self.recs.get(k, ()):
                if r[0] < p1 and p0 < r[1] and r[2] < b1 and b0 < r[3]:
                    deps.add(r[4])
        if dma:
            n = self.n_pool
            sw = (eng == 'pool')
            cnt = self.dcount[sw]
            o.dsem = (cnt % n) + (n if sw else 0); o.dval = 16 * (cnt // n + 1)
            if cnt >= n:
                deps.add(self.dhist[sw][cnt - n])
            self.dhist[sw].append(idx); self.dcount[sw] += 1
            self.dma_ops.append(idx); self.ndma += 1
        bi = getattr(self, 'barrier_idx', 0)
        if bi:
            last = {}
            for j in range(bi):
                last[(self.ops[j].eng, self.ops[j].is_dma and j)] = j
            deps |= set(last.values())
        o.odeps = set()
        if eng == 'pe':
            o.odeps = {d for d in deps if (self.ops[d].eng == 'pe' and not self.ops[d].is_dma)}
            deps = deps - o.odeps
        o.deps = deps
        o.idx = idx
        o.cost, o.nbytes = self._cost(eng, dma, outs, ins)
        self.ops.append(o)
        for (k, p0, p1, b0, b1) in wregs:
            lst = self.recs.setdefault(k, [])
            lst[:] = [r for r in lst if not (p0 <= r[0] and r[1] <= p1 and b0 <= r[2] and r[3] <= b1)]
            lst.append([p0, p1, b0, b1, idx, True])
        for (k, p0, p1, b0, b1) in rregs:
            lst = self.recs.setdefault(k, [])
            lst.append([p0, p1, b0, b1, idx, False])
        return o

    def _cost(self, eng, dma, outs, ins):
        def fsz(a):
            n = 1
            for v in list(a.shape)[1:]: n *= int(v)
            return n
        if dma:
            nb = fsz(outs[0]) * int(outs[0].shape[0]) * max(mybir.dt.size(outs[0].dtype), mybir.dt.size(ins[0].dtype))
            return 2000.0 + nb / 250.0, nb
        if eng == 'pe':
            rhs = ins[1]
            n = max(64, fsz(rhs))
            c = n * 0.42
            if mybir.dt.size(rhs.dtype) == 4: c *= 4
            return c + 10.0, 0
        f = fsz(outs[0]) if outs else 16
        if eng == 'act': return 220.0 + f * 0.72, 0
        if eng == 'dve': return 80.0 + f * 1.05, 0
        return 150.0 + f * 2.1, 0

    def schedule(self, window=600, lat_x=900.0, lat_s=250.0):
        import heapq
        ops = self.ops; n = len(ops)
        succ = [[] for _ in range(n)]
        indeg = [0] * n
        for o in ops:
            ds = o.deps | o.odeps
            indeg[o.idx] = len(ds)
            for d in ds: succ[d].append(o.idx)
        fin = [0.0] * n
        rt = [0.0] * n
        tcur = {e: 0.0 for e in self.ENGS}
        dma_bw = [0.0]
        later = {e: [] for e in self.ENGS}
        nowq = {e: [] for e in self.ENGS}
        released = [False] * n
        waiting = set()
        base = 0; done = [False] * n; order = []
        def push(i):
            heapq.heappush(later[ops[i].eng], (rt[i], i))
        for i in range(n):
            if indeg[i] == 0:
                if i < window: push(i)
                else: waiting.add(i)
        nsched = 0
        while nsched < n:
            best = None
            for e in self.ENGS:
                L = later[e]; Q = nowq[e]
                while L and L[0][0] <= tcur[e]:
                    heapq.heappush(Q, heapq.heappop(L)[1])
                if Q:
                    cand = (tcur[e], Q[0], e, True)
                elif L:
                    cand = (L[0][0], L[0][1], e, False)
                else:
                    continue
                if best is None or cand[:2] < best[:2]: best = cand
            if best is None:
                i = min(waiting); waiting.discard(i); push(i); continue
            st, i, e, fromq = best
            if fromq: heapq.heappop(nowq[e])
            else: heapq.heappop(later[e])
            o = ops[i]
            if o.is_dma:
                issue = st
                s0 = max(issue + 600.0, dma_bw[0])
                dma_bw[0] = s0 + o.nbytes / 250.0
                fin[i] = s0 + 1400.0 + o.nbytes / 250.0
                tcur[e] = issue + 60.0
            else:
                fin[i] = st + o.cost
                tcur[e] = fin[i]
            done[i] = True; order.append(i); nsched += 1
            while base < n and done[base]: base += 1
            for j in succ[i]:
                lt = lat_s if (ops[j].eng == e and not o.is_dma) else lat_x
                if i in ops[j].odeps and i not in ops[j].deps: lt = 0.0
                if fin[i] + lt > rt[j]: rt[j] = fin[i] + lt
                indeg[j] -= 1
                if indeg[j] == 0:
                    if j < base + window: push(j)
                    else: waiting.add(j)
            if waiting:
                mv = [j for j in waiting if j < base + window]
                for j in mv:
                    waiting.discard(j); push(j)
        self.est_time = max(fin) if fin else 0.0
        return order

    def emit(self):
        import os as _o
        if _o.environ.get('KNOSCHED'):
            order = list(range(len(self.ops)))
        else:
            order = self.schedule(window=int(_o.environ.get('KWIN', '3200')))
        ops_all = self.ops
        cntp = {e: 0 for e in self.ENGS}
        for i in order:
            ops_all[i].pos = cntp[ops_all[i].eng]; cntp[ops_all[i].eng] += 1
        ops = [ops_all[i] for i in order]
        self._all = ops_all
        waited_pos = {e: {f: -1 for f in self.ENGS} for e in self.ENGS}
        waited_dma = {e: {} for e in self.ENGS}
        for o in ops:
            need_pos = {}
            need_dma = {}
            for d in o.deps:
                a = ops_all[d]
                if a.is_dma:
                    if need_dma.get(a.dsem, 0) < a.dval: need_dma[a.dsem] = a.dval
                else:
                    if need_pos.get(a.eng, -1) < a.pos: need_pos[a.eng] = a.pos
            w = []
            for f, p in need_pos.items():
                if waited_pos[o.eng][f] < p:
                    waited_pos[o.eng][f] = p
                    w.append(('e', f, p))
            for s, v in need_dma.items():
                if waited_dma[o.eng].get(s, 0) < v:
                    waited_dma[o.eng][s] = v
                    w.append(('d', s, v))
            o.waits = w
        by_eng_pos = {e: {} for e in self.ENGS}
        for o in ops:
            if not o.is_dma: by_eng_pos[o.eng][o.pos] = o
        for o in ops:
            for w in o.waits:
                if w[0] == 'e': by_eng_pos[w[1]][w[2]].signal = True
        seqc = {e: 0 for e in self.ENGS}
        for o in ops:
            if (not o.is_dma) and o.signal:
                seqc[o.eng] += 1
            o.seq = seqc[o.eng]
        nw = 0
        for o in ops:
            E = self.eng[o.eng]
            for w in o.waits:
                if w[0] == 'e':
                    E.wait_ge(self.sem[w[1]], by_eng_pos[w[1]][w[2]].seq)
                else:
                    E.wait_ge(self.dsems[w[1]], w[2])
                nw += 1
            ins = o.fn()
            if o.is_dma:
                ins.then_inc(self.dsems[o.dsem], 16)
            elif o.signal:
                ins.then_inc(self.sem[o.eng], 1)
        _sw = 0; _last = None
        for o in ops:
            if o.eng == 'act' and o.tag and any(k in o.tag for k in ('Exp', 'Ln', 'Silu', 'Sqrt')):
                grp = 'E' if ('Exp' in o.tag or 'Ln' in o.tag) else o.tag
                if grp != _last: _sw += 1; _last = grp
        self.act_switches = _sw
        self.stats = dict(n_ops=len(ops), n_waits=nw, sig={e: seqc[e] for e in self.ENGS}, cnt=dict(self.cnt), act_switches=_sw)
        return self.stats

    def final_wait(self, eng='sp'):
        E = self.eng[eng]
        last = {}
        for idx in self.dma_ops:
            o = self.ops[idx]
            last[o.dsem] = max(last.get(o.dsem, 0), o.dval)
        for s, v in last.items():
            E.wait_ge(self.dsems[s], v)

    def barrier(self):
        self.barrier_idx = len(self.ops)
        self.recs_barrier = True
    def dma(self, q, out, in_, **kw):
        E = self.eng[q]
        return self.op(q, lambda: E.dma_start(out=out, in_=in_, **kw), [out], [in_], dma=True)
    def mm(self, out, lhsT, rhs, start=True, stop=True, **kw):
        nc = self.nc
        return self.op('pe', lambda: nc.tensor.matmul(out, lhsT=lhsT, rhs=rhs, start=start, stop=stop, **kw), [out], [lhsT, rhs])
    def tr(self, out, in_, ident):
        nc = self.nc
        return self.op('pe', lambda: nc.tensor.transpose(out, in_, ident), [out], [in_, ident])
    def act(self, out, in_, func, bias=None, scale=1.0, accum_out=None, eng='act'):
        nc = self.nc
        kw = {}
        if bias is not None: kw['bias'] = bias
        if accum_out is not None: kw['accum_out'] = accum_out
        ins = [in_] + [x for x in (bias, scale) if hasattr(x, 'tensor')]
        outs = [out] + ([accum_out] if accum_out is not None else [])
        o = self.op('act', lambda: nc.scalar.activation(out=out, in_=in_, func=func, scale=scale, **kw), outs, ins)
        if o is not None: o.tag = str(func)
        return o
    def tt(self, eng, out, in0, in1, op):
        E = self.eng[eng]
        return self.op(eng, lambda: E.tensor_tensor(out=out, in0=in0, in1=in1, op=op), [out], [in0, in1])
    def ts(self, eng, out, in0, s1, s2, op0, op1=None, accum_out=None):
        E = self.eng[eng]
        kw = {}
        if op1 is not None: kw['op1'] = op1
        if accum_out is not None: kw['accum_out'] = accum_out
        ins = [in0] + [x for x in (s1, s2) if hasattr(x, 'tensor')]
        outs = [out] + ([accum_out] if accum_out is not None else [])
        return self.op(eng, lambda: E.tensor_scalar(out=out, in0=in0, scalar1=s1, scalar2=s2, op0=op0, **kw), outs, ins)
    def stt(self, eng, out, in0, scalar, in1, op0, op1, accum_out=None):
        E = self.eng[eng]
        kw = {}
        if accum_out is not None: kw['accum_out'] = accum_out
        ins = [in0, in1] + ([scalar] if hasattr(scalar, 'tensor') else [])
        outs = [out] + ([accum_out] if accum_out is not None else [])
        return self.op(eng, lambda: E.scalar_tensor_tensor(out=out, in0=in0, scalar=scalar, in1=in1, op0=op0, op1=op1, **kw), outs, ins)
    def copy(self, eng, out, in_):
        if eng == 'act':
            return self.act(out, in_, AF.Copy)
        E = self.eng[eng]
        return self.op(eng, lambda: E.tensor_copy(out=out, in_=in_), [out], [in_])
    def memset(self, eng, out, val):
        E = self.eng[eng]
        return self.op(eng, lambda: E.memset(out, val), [out], [])
    def reduce(self, eng, out, in_, op, axis=AX.X):
        E = self.eng[eng]
        return self.op(eng, lambda: E.tensor_reduce(out=out, in_=in_, axis=axis, op=op), [out], [in_])
    def recip(self, out, in_):
        nc = self.nc
        return self.op('dve', lambda: nc.vector.reciprocal(out=out, in_=in_), [out], [in_])
    def aselect(self, out, in_, pattern, compare_op, fill, base, channel_multiplier):
        nc = self.nc
        return self.op('pool', lambda: nc.gpsimd.affine_select(out=out, in_=in_, pattern=pattern, compare_op=compare_op,
                       fill=fill, base=base, channel_multiplier=channel_multiplier), [out], [in_])

import contextlib
import math

D = 2048; NK = 16
SEQ = 2048; HALF = 1024; NSB = 16; TS = 64; TALL = HALF + TS
D_STATE = 128; NH = 16; HD = 64
PAST_LEN = 16384
D_FF = 5632
EPS = 1e-5
COL_Z, COL_XBC, COL_DT, COL_Q, COL_K, COL_V = 0, 1024, 2560, 2576, 3600, 3728
IN_DIM = 3856

C_ID, C_TRI, C_U, C_ONES, C_TRI4, C_U4, C_BLK4, C_OH, C_MC, C_SELB = 0, 128, 256, 384, 512, 576, 640, 704, 720, 724
NCONST = 724 + 1024
NPC = 10 * 16 + 1

def _sz(dt):
    return mybir.dt.size(dt)

class Arena:
    def __init__(self, nc, es, nbytes):
        self.t = es.enter_context(nc.sbuf_tensor("arena", [128, nbytes // 4], F32))
        self.cap = nbytes; self.top = 0; self.peak = 0; self.limit = nbytes
    def alloc(self, shape, dtype):
        n = 1
        for s in shape[1:]: n *= s
        nb = n * _sz(dtype)
        nb4 = (nb + 31) // 32 * 32
        st = self.top
        assert st + nb4 <= self.limit, ("arena overflow", st, nb4, self.limit)
        self.top += nb4; self.peak = max(self.peak, self.top)
        ap = self.t[0:shape[0], st // 4:(st + nb4) // 4]
        if dtype != F32:
            ap = ap.bitcast(dtype)
        ap = ap[:, 0:n]
        if len(shape) == 2:
            return ap
        names = " ".join("d%d" % i for i in range(1, len(shape)))
        kw = {"d%d" % i: shape[i] for i in range(1, len(shape))}
        return ap.rearrange("p (%s) -> p %s" % (names, names), **kw)
    def alloc_at(self, offset, shape, dtype):
        save = (self.top, self.limit, self.peak)
        self.top = offset; self.limit = self.cap
        ap = self.alloc(shape, dtype)
        self.top, self.limit = save[0], save[1]
        return ap
    def mark(self): return self.top
    def release(self, m): self.top = m

def bc(ap, shape):
    return ap.to_broadcast(shape)

class Rot:
    def __init__(self, items): self.items = list(items); self.i = 0
    def next(self):
        x = self.items[self.i % len(self.items)]; self.i += 1; return x

def build_nc(dbg=()):
    nc = bass.Bass("TRN2", target_bir_lowering=False)
    def din(name, shape): return nc.dram_tensor(name, list(shape), F32, kind="ExternalInput").ap()
    def dout(name, shape): return nc.dram_tensor(name, list(shape), F32, kind="ExternalOutput").ap()
    xo = din("xo", [HALF, D]); xpv = din("xpv", [HALF, D]); xs = din("xs", [TS, D]); mem = din("mem", [256, D])
    sssm = din("sssm", [NSB, 1024, 128]); sconv = din("sconv", [48, 1536])
    ck = din("ck", [NSB, 128, 128]); cv = din("cv", [NSB, 128, 128])
    cmk = din("cmk", [NSB, 256, 512]); cmv = din("cmv", [NSB, 256, 512])
    consts = din("consts", [128, NCONST]); pc = din("pc", [128, NPC]); nrm = din("nrm", [128, 64])
    convw = din("convw", [128, 60]); hv = din("hv", [4, 16]); gnorm = din("gnorm", [1024]); nfin = din("nfin", [D])
    w_in = din("w_in", [D, IN_DIM]); w_out = din("w_out", [D, D])
    w_xq = din("w_xq", [D, 512]); w_xk = din("w_xk", [D, 512]); w_xv = din("w_xv", [D, 512]); w_xo = din("w_xo", [512, D])
    w_gate = din("w_gate", [D, D_FF]); w_up = din("w_up", [D, D_FF]); w_down = din("w_down", [D_FF, D])
    y = dout("y", [HALF, D]); ys = dout("ys", [TS, D])
    o_ssm = dout("o_ssm", [1024, 128]); o_conv = dout("o_conv", [3, 1536]); o_k = dout("o_k", [128, 128]); o_v = dout("o_v", [128, 128])
    o_mk = dout("o_mk", [256, 512]); o_mv = dout("o_mv", [256, 512])
    s_ssm = dout("s_ssm", [NSB, 1024, 128]); s_conv = dout("s_conv", [48, 1536]); s_k = dout("s_k", [NSB, 128, 128]); s_v = dout("s_v", [NSB, 128, 128])
    dbg_out = {}

    es = contextlib.ExitStack()
    with es:
        A = Arena(nc, es, 207 * 1024)
        ps_all = es.enter_context(nc.psum_tensor("ps", [128, 4096], F32))
        S = Sched(nc, es)
        def bank(i): return ps_all[:, i * 512:(i + 1) * 512]
        def bankb(i): return ps_all[:, i * 512:(i + 1) * 512].bitcast(BF16)
        def dump(name, ap, shape):
            if name in dbg:
                o = dout("dbg_" + name, shape)
                S.dma('sp' if ap.dtype == F32 else 'pool', o, ap)

        cst = A.alloc([128, NCONST], F32); S.dma('sp', cst, consts)
        pcs = A.alloc([128, NPC], F32); S.dma('sp', pcs, pc)
        nrms = A.alloc([128, 4, 16], F32); S.dma('sp', nrms, nrm.rearrange("p (a b) -> p a b", a=4))
        cw = A.alloc([128, 12, 5], F32); S.dma('sp', cw, convw.rearrange("p (a b) -> p a b", a=12))
        hvs = A.alloc([128, 4, 16], F32)
        for i in range(4):
            S.dma('sp', hvs[:, i, :], hv[i].partition_broadcast(128))
        gn = A.alloc([128, 1024], F32); S.dma('sp', gn, gnorm.partition_broadcast(128))
        identf = cst[:, C_ID:C_ID + 128]; tri = cst[:, C_TRI:C_TRI + 128]; Um = cst[:, C_U:C_U + 128]; onesf = cst[:, C_ONES:C_ONES + 128]
        tri4 = cst[:, C_TRI4:C_TRI4 + 64]; U4 = cst[:, C_U4:C_U4 + 64]; blk4 = cst[:, C_BLK4:C_BLK4 + 64]
        onehot = cst[:, C_OH:C_OH + 16]; maskc = cst[:, C_MC:C_MC + 4]
        selb = cst[:, C_SELB:C_SELB + 1024].rearrange("p (b t) -> p b t", b=16)
        one_col = cst[:, C_ONES:C_ONES + 1]
        identb = A.alloc([128, 128], BF16); S.copy('dve', identb, identf)
        onesb = A.alloc([128, 128], BF16); S.copy('dve', onesb, onesf)
        eps_t = A.alloc([128, 1], F32); S.memset('dve', eps_t, EPS)
        a_neg = A.alloc([128, 16], F32); S.act(a_neg, hvs[:, 1, :], AF.Exp); S.ts('dve', a_neg, a_neg, -1.0, None, ALU.mult)
        esink = A.alloc([128, 16], F32); S.act(esink, hvs[:, 3, :], AF.Exp)
        dtb = hvs[:, 0, :]; dskip = hvs[:, 2, :]
        flag = pcs[:, 160:161]
        def ropec(tile, rows): return pcs[0:rows, tile * 16:tile * 16 + 8]
        def ropes(tile, rows): return pcs[0:rows, tile * 16 + 8:tile * 16 + 16]
        small = A.alloc([128, 64], F32)
        ss_r = Rot([small[:, i:i + 1] for i in range(0, 4)])
        std_r = Rot([small[:, i:i + 1] for i in range(4, 8)])
        rstd_r = Rot([small[:, i:i + 1] for i in range(8, 12)])

        psr = Rot(range(8))

        state = A.alloc([128, 1024], F32)
        halo_prev = A.alloc([128, 12, 3], F32)
        kT = A.alloc([128, 128 + TALL], BF16)
        v_tok = A.alloc([128, 10, 2, 65], BF16)
        S.memset('pool', v_tok[:, :, :, 64:65], 1.0)
        wT_io = w_in.rearrange("(k p) n -> p k n", p=128)

        def load_w(dst, src3, c0, n, dcol=0):
            K = src3.shape[1]
            step = 4
            for k0 in range(0, K, step):
                k1 = min(K, k0 + step)
                S.dma('pool', dst[:, k0:k1, dcol:dcol + n], src3[:, k0:k1, c0:c0 + n])

        def norm_T(x_sb, rows, wT, dst, col0, xn, junk, scale_eng='act'):
            ss = ss_r.next(); std = std_r.next(); rstd = rstd_r.next()
            S.act(junk[0:rows], x_sb, AF.Square, accum_out=ss[0:rows])
            S.act(std[0:rows], ss[0:rows], AF.Sqrt, scale=1.0 / D, bias=eps_t[0:rows])
            S.recip(rstd[0:rows], std[0:rows])
            if scale_eng == 'act':
                S.act(xn[0:rows], x_sb, AF.Copy, scale=rstd[0:rows])
            else:
                S.ts('dve', xn[0:rows], x_sb, rstd[0:rows], None, ALU.mult)
            for half in range(2):
                pb = bankb(psr.next()).rearrange("p (k t) -> p k t", k=8)
                for k in range(8):
                    kk = half * 8 + k
                    S.tr(pb[:, k, 0:rows], xn[0:rows, kk * 128:(kk + 1) * 128], identb[0:rows, 0:rows])
                S.tt('dve', dst[:, half * 8:half * 8 + 8, col0:col0 + rows], pb[:, :, 0:rows],
                     bc(wT[:, half * 8:half * 8 + 8].unsqueeze(2), [128, 8, rows]), ALU.mult)

        def dt_proc(ps16, rows, dt_out, dtA_out, tmp):
            ta = tmp[0:rows, 0:16]; tb = tmp[0:rows, 16:32]
            S.tt('dve', ta, ps16, dtb[0:rows], ALU.add)
            S.act(tb, ta, AF.Abs)
            S.act(tb, tb, AF.Exp, scale=-1.0)
            S.act(tb, tb, AF.Ln, bias=one_col[0:rows])
            S.ts('dve', ta, ta, 0.0, None, ALU.max)
            S.tt('dve', dt_out, ta, tb, ALU.add)
            S.tt('dve', dtA_out, dt_out, a_neg[0:rows], ALU.mult)

        def rope_apply(ps3, dst3, rows, nh, tile, tmp):
            c = bc(ropec(tile, rows).unsqueeze(1), [rows, nh, 8]); s = bc(ropes(tile, rows).unsqueeze(1), [rows, nh, 8])
            def tv(o): return tmp[0:rows, o:o + nh * 8].rearrange("p (h d) -> p h d", h=nh)
            x1 = dst3[:, :, 0:8]; x2 = dst3[:, :, 8:16]
            S.tt('dve', tv(0), x1, c, ALU.mult); S.tt('dve', tv(128), x2, s, ALU.mult)
            S.tt('dve', tv(256), x2, c, ALU.mult); S.tt('dve', tv(384), x1, s, ALU.mult)
            S.tt('dve', x1, tv(0), tv(128), ALU.subtract)
            S.tt('dve', x2, tv(256), tv(384), ALU.add)

        def conv4(acc, src, cc, n, three_d=False, eng='dve'):
            def sl(k): return src[:, :, k:k + n] if three_d else src[:, k:k + n]
            S.ts(eng, acc, sl(0), cw[:, cc, 0:1], cw[:, cc, 4:5], ALU.mult, ALU.add)
            for k in range(1, 4):
                S.stt(eng, acc, sl(k), cw[:, cc, k:k + 1], acc, ALU.mult, ALU.add)

        t16 = A.alloc([128, 96], F32)
        def chunk_state(xs_t, Bt, dt_c, dtA_c, rows, first, tri_m, ones_m, xw):
            pb = bank(psr.next())
            cs_ps = pb[0:rows, 0:16]; tot_ps = pb[0:rows, 16:32]
            S.mm(cs_ps, tri_m, dtA_c)
            S.mm(tot_ps, ones_m, dtA_c)
            cs_sb = t16[0:rows, 0:16]; tot_sb = t16[0:rows, 16:32]; wend = t16[0:rows, 32:48]; dec = t16[0:rows, 48:64]
            S.act(cs_sb, cs_ps, AF.Copy)
            S.act(tot_sb, tot_ps, AF.Copy)
            S.tt('dve', wend, tot_sb, cs_sb, ALU.subtract)
            S.act(wend, wend, AF.Exp)
            S.tt('dve', wend, wend, dt_c, ALU.mult)
            S.act(dec, tot_sb, AF.Exp)
            S.tt('dve', xw[0:rows].rearrange("p (h d) -> p h d", h=16), xs_t.rearrange("p (h d) -> p h d", h=16),
                 bc(wend.unsqueeze(2), [rows, 16, 64]), ALU.mult)
            return cs_sb, tot_sb, dec

        def state_update(Bt, xw, rows, dec, first):
            for g in range(2):
                pbk = bank(psr.next())
                S.mm(pbk, Bt[:, g * 128:(g + 1) * 128], xw[0:rows, g * 512:(g + 1) * 512])
                sg = state[:, g * 512:(g + 1) * 512]
                if first:
                    S.act(sg, pbk, AF.Copy)
                else:
                    S.tt('dve', sg.rearrange("p (h d) -> p h d", h=8), sg.rearrange("p (h d) -> p h d", h=8),
                         bc(dec[:, g * 8:(g + 1) * 8].unsqueeze(2), [128, 8, 64]), ALU.mult)
                    S.tt('dve', sg, sg, pbk, ALU.add)

        m_prev = A.mark()
        hTp = A.alloc([128, NK, HALF], BF16)
        xin = Rot([A.alloc([128, D], F32) for _ in range(2)])
        xnb = Rot([A.alloc([128, D], BF16) for _ in range(2)])
        junk = A.alloc([128, D], BF16)
        wblk = Rot([A.alloc([128, NK, 512], BF16) for _ in range(2)])
        cbuf = Rot([A.alloc([128, 3 + HALF], F32) for _ in range(2)])
        cacc = Rot([A.alloc([128, HALF], F32) for _ in range(2)])
        xcb = Rot([A.alloc([128, TALL], BF16) for _ in range(2)])
        xs_p = A.alloc([128, 8, 1024], BF16)
        Bt_p = A.alloc([128, 8, 256], BF16)
        dt_p = A.alloc([128, 8, 16], F32); dtA_p = A.alloc([128, 8, 16], F32)
        tmp_r = Rot([A.alloc([128, 512], F32) for _ in range(2)])
        xw_r = Rot([A.alloc([128, 1024], BF16) for _ in range(2)])

        for i in range(8):
            xt = xin.next(); S.dma('sp', xt, xpv[i * 128:(i + 1) * 128, :])
            norm_T(xt, 128, nrms[:, 0, :], hTp, i * 128, xnb.next(), junk)
        HT_TOP = A.cap - NK * TALL * 2
        hT = A.alloc_at(HT_TOP, [128, NK, TALL], BF16)
        A.limit = HT_TOP
        for i in range(9):
            rows = 128 if i < 8 else TS
            xt = xin.next()
            S.dma('sp', xt[0:rows], xo[i * 128:(i + 1) * 128, :] if i < 8 else xs[:, :])
            xn = xnb.next()
            norm_T(xt[0:rows], rows, nrms[:, 0, :], hT, i * 128, xn, xn)
        wb = wblk.next()
        load_w(wb, wT_io, COL_DT, 16)
        load_w(wb, wT_io, COL_K, 256, dcol=16)
        for i in range(8):
            pb = bank(psr.next())
            for k in range(NK):
                S.mm(pb[:, 0:16], hTp[:, k, i * 128:(i + 1) * 128], wb[:, k, 0:16], start=(k == 0), stop=(k == NK - 1))
            dt_proc(pb[:, 0:16], 128, dt_p[:, i, :], dtA_p[:, i, :], tmp_r.next())
        def kv_tile(hT_src, c0, rows, wbk, wcol, tile, kcol, kf, vf):
            pb = bank(psr.next())
            for k in range(NK):
                S.mm(pb[0:rows, 0:256], hT_src[:, k, c0:c0 + rows], wbk[:, k, wcol:wcol + 256], start=(k == 0), stop=(k == NK - 1))
            S.act(kf[0:rows], pb[0:rows, 0:128], AF.Copy)
            rope_apply(pb[0:rows, 0:128].rearrange("p (h d) -> p h d", h=2), kf[0:rows].rearrange("p (h d) -> p h d", h=2), rows, 2, tile, tmp_r.next())
            S.act(vf[0:rows], pb[0:rows, 128:256], AF.Copy)
            kb = xnb.next()[0:rows, 0:128]
            S.copy('dve', kb, kf[0:rows])
            pt = bankb(psr.next())
            S.tr(pt[:, 0:rows], kb, identb[0:rows, 0:rows])
            S.copy('dve', kT[:, kcol:kcol + rows], pt[:, 0:rows])
            S.copy('dve', v_tok[0:rows, tile, :, 0:64], vf[0:rows].rearrange("p (g d) -> p g d", g=2))
        kvf = A.alloc([128, 2, 128], F32)
        kv_tile(hTp, 7 * 128, 128, wb, 16, 9, 0, kvf[:, 0, :], kvf[:, 1, :])
        for b in range(3):
            wb = wblk.next()
            load_w(wb, wT_io, COL_XBC + b * 512, 512)
            for j in range(4):
                cc = 4 * b + j
                cb = cbuf.next()
                S.memset('dve', cb[:, 0:3], 0.0)
                for g in range(2):
                    pb = bank(psr.next())
                    for k in range(NK):
                        S.mm(pb, wb[:, k, j * 128:(j + 1) * 128], hTp[:, k, g * 512:(g + 1) * 512], start=(k == 0), stop=(k == NK - 1))
                    S.act(cb[:, 3 + g * 512:3 + (g + 1) * 512], pb, AF.Copy)
                S.copy('dve', halo_prev[:, cc, :], cb[:, HALF:HALF + 3])
                if cc >= 10:
                    continue
                acc = cacc.next(); xc = xcb.next()
                conv4(acc, cb, cc, HALF)
                S.act(xc[:, 0:HALF], acc, AF.Silu)
                pt = bankb(psr.next()).rearrange("p (t c) -> p t c", t=8)
                for t in range(8):
                    S.tr(pt[:, t, :], xc[:, t * 128:(t + 1) * 128], identb)
                if cc < 8:
                    S.copy('dve', xs_p[:, :, cc * 128:(cc + 1) * 128], pt)
                else:
                    S.copy('dve', Bt_p[:, :, (cc - 8) * 128:(cc - 7) * 128], pt)
        for c in range(8):
            xw = xw_r.next()
            cs_sb, tot_sb, dec = chunk_state(xs_p[:, c, :], Bt_p[:, c, :], dt_p[:, c, :], dtA_p[:, c, :], 128, c == 0, tri, onesf, xw)
            state_update(Bt_p[:, c, :], xw, 128, dec, c == 0)
        S.ts('dve', state, state, flag, None, ALU.mult)
        dump("state8", state, [128, 1024]); dump("halo", halo_prev.rearrange("p a b -> p (a b)"), [128, 36])
        dump("kTp", kT[:, 0:128], [128, 128])
        A.release(m_prev)
        def trows(i): return 128 if i < 8 else TS
        import os as _os
        m_own = A.mark()
        xs_t = A.alloc([128, 9, 1024], BF16)
        Bt = A.alloc([128, 9, 256], BF16)
        BT = A.alloc([128, 2, TALL], BF16)
        CT = A.alloc([128, 2, TALL], BF16)
        dt_o = A.alloc([128, 9, 16], F32); dtA_o = A.alloc([128, 9, 16], F32)
        zs = A.alloc([128, 9, 1024], BF16)
        qT = A.alloc([128, 8, TALL], BF16)
        kvo = A.alloc([128, 4, 128], F32)
        m_inproj = A.mark()
        wblk = Rot([A.alloc([128, NK, 512], BF16) for _ in range(2)])
        tmp_r = Rot([A.alloc([128, 512], F32) for _ in range(2)])
        S.cutpoint('o1')
        cbuf = Rot([A.alloc([128, 3 + HALF], F32) for _ in range(2)])
        cacc = Rot([A.alloc([128, HALF], F32) for _ in range(2)])
        xcb = Rot([A.alloc([128, TALL], BF16) for _ in range(2)])
        pcv = Rot([A.alloc([128, 512], F32) for _ in range(2)])
        cbs = A.alloc([128, 16, 7], F32); accs = A.alloc([128, 16, 4], F32)
        scs = A.alloc([48, 1536], F32); S.dma('sp', scs, sconv)
        kbb = Rot([A.alloc([128, 128], BF16) for _ in range(2)])
        qbf = Rot([A.alloc([128, 512], BF16) for _ in range(2)])
        qff = Rot([A.alloc([128, 512], F32) for _ in range(2)])

        wb = wblk.next()
        load_w(wb, wT_io, COL_DT, 16)
        load_w(wb, wT_io, COL_K, 256, dcol=16)
        for i in range(9):
            rows = trows(i)
            pb = bank(psr.next())
            for k in range(NK):
                S.mm(pb[0:rows, 0:16], hT[:, k, i * 128:i * 128 + rows], wb[:, k, 0:16], start=(k == 0), stop=(k == NK - 1))
            dt_proc(pb[0:rows, 0:16], rows, dt_o[0:rows, i, :], dtA_o[0:rows, i, :], tmp_r.next())
        S.cutpoint('dt')
        kvcount = [0]
        def kv_tile2(i, rows, kf, vf):
            steps = _os.environ.get('KVS', '1234')
            if ',' in steps:
                steps = steps.split(',')[kvcount[0]]; kvcount[0] += 1
            pb = bank(psr.next())
            for k in range(NK):
                S.mm(pb[0:rows, 0:256], hT[:, k, i * 128:i * 128 + rows], wb[:, k, 16:272], start=(k == 0), stop=(k == NK - 1))
            S.act(kf[0:rows], pb[0:rows, 0:128], AF.Copy)
            if '2' in steps:
                rope_apply(pb[0:rows, 0:128].rearrange("p (h d) -> p h d", h=2), kf[0:rows].rearrange("p (h d) -> p h d", h=2), rows, 2, i, tmp_r.next())
            S.act(vf[0:rows], pb[0:rows, 128:256], AF.Copy)
            if '3' in steps:
                kb = kbb.next()[0:rows]
                S.copy('dve', kb, kf[0:rows])
                pt = bankb(psr.next())
                S.tr(pt[:, 0:rows], kb, identb[0:rows, 0:rows])
                S.copy('dve', kT[:, 128 + i * 128:128 + i * 128 + rows], pt[:, 0:rows])
            if '4' in steps:
                S.copy('dve', v_tok[0:rows, i, :, 0:64], vf[0:rows].rearrange("p (g d) -> p g d", g=2))
        kvtmp = Rot([A.alloc([128, 2, 128], F32) for _ in range(2)])
        if _os.environ.get('KROT'):
            for _r in _os.environ['KROT']:
                {'t': tmp_r, 'k': kbb, 'v': kvtmp, 'p': psr}[_r].next()
        for i in [int(t) for t in _os.environ.get('KVT', '0,1,2,3,4,5,6,7,8').split(',')]:
            rows = trows(i)
            if i == 7: kf, vf = kvo[:, 0, :], kvo[:, 1, :]
            elif i == 8: kf, vf = kvo[:, 2, :], kvo[:, 3, :]
            else:
                t = kvtmp.next(); kf, vf = t[:, 0, :], t[:, 1, :]
            kv_tile2(i, rows, kf, vf)
            if _os.environ.get('KBAR'): S.barrier()
        for _d in range(int(_os.environ.get('KDUM', '0'))):
            S.memset('dve', small[:, 40:41], 0.0)
        S.cutpoint('kv')
        S.dma('sp', o_k, kvo[:, 0, :]); S.dma('sp', o_v, kvo[:, 1, :])
        for l in range(4):
            S.dma('sp', s_k[:, 124 + l, :], kvo[l:TS:4, 2, :])
            S.dma('sp', s_v[:, 124 + l, :], kvo[l:TS:4, 3, :])

        S.cutpoint('kvout')
        o_conv_v = o_conv
        s_conv_v = s_conv.rearrange("(b k) c -> b k c", k=3)
        for b in range(3):
            wb = wblk.next()
            load_w(wb, wT_io, COL_XBC + b * 512, 512)
            for (i, rows) in ((7, 128), (8, TS)):
                pb = bank(psr.next())
                for k in range(NK):
                    S.mm(pb[0:rows], hT[:, k, i * 128:i * 128 + rows], wb[:, k, :], start=(k == 0), stop=(k == NK - 1))
                pv = pcv.next()
                S.act(pv[0:rows], pb[0:rows], AF.Copy)
                if i == 7:
                    S.dma('sp', o_conv_v[:, b * 512:(b + 1) * 512], pv[125:128, :])
                else:
                    for l in range(1, 4):
                        S.dma('sp', s_conv_v[:, l - 1, b * 512:(b + 1) * 512], pv[l:TS:4, :])
            for j in range(4):
                cc = 4 * b + j
                cb = cbuf.next()
                S.copy('dve', cb[:, 0:3], halo_prev[:, cc, :])
                for g in range(2):
                    pb = bank(psr.next())
                    for k in range(NK):
                        S.mm(pb, wb[:, k, j * 128:(j + 1) * 128], hT[:, k, g * 512:(g + 1) * 512], start=(k == 0), stop=(k == NK - 1))
                    S.act(cb[:, 3 + g * 512:3 + (g + 1) * 512], pb, AF.Copy)
                pb = bank(psr.next())
                for k in range(NK):
                    S.mm(pb[:, 0:TS], wb[:, k, j * 128:(j + 1) * 128], hT[:, k, HALF:TALL], start=(k == 0), stop=(k == NK - 1))
                S.act(cbs[:, :, 3:7], pb[:, 0:TS].rearrange("p (b l) -> p b l", b=16), AF.Copy)
                pbs = bank(psr.next())
                S.tr(pbs[:, 0:48], scs[:, cc * 128:(cc + 1) * 128], identf[0:48, 0:48])
                S.act(cbs[:, :, 0:3], pbs[:, 0:48].rearrange("p (b k) -> p b k", b=16), AF.Copy)
                acc = cacc.next()
                conv4(acc, cb, cc, HALF)
                conv4(accs, cbs, cc, 4, three_d=True)
                if cc < 8:
                    dst = xcb.next()
                elif cc < 10:
                    dst = BT[:, cc - 8, :]
                else:
                    dst = CT[:, cc - 10, :]
                S.act(dst[:, 0:HALF], acc, AF.Silu)
                S.act(dst[:, HALF:TALL].rearrange("p (b l) -> p b l", b=16), accs, AF.Silu)
                if cc < 10:
                    pt = bankb(psr.next()).rearrange("p (t c) -> p t c", t=8)
                    for t in range(8):
                        S.tr(pt[:, t, :], dst[:, t * 128:(t + 1) * 128], identb)
                    pt2 = bankb(psr.next())
                    S.tr(pt2[0:TS, 0:128], dst[:, HALF:TALL], identb)
                    if cc < 8:
                        S.copy('dve', xs_t[:, 0:8, cc * 128:(cc + 1) * 128], pt)
                        S.copy('dve', xs_t[0:TS, 8, cc * 128:(cc + 1) * 128], pt2[0:TS, 0:128])
                    else:
                        S.copy('dve', Bt[:, 0:8, (cc - 8) * 128:(cc - 7) * 128], pt)
                        S.copy('dve', Bt[0:TS, 8, (cc - 8) * 128:(cc - 7) * 128], pt2[0:TS, 0:128])
        S.cutpoint('xbc')
        for b in range(2):
            wb = wblk.next()
            load_w(wb, wT_io, COL_Z + b * 512, 512)
            for i in range(9):
                rows = trows(i)
                pb = bank(psr.next())
                for k in range(NK):
                    S.mm(pb[0:rows], hT[:, k, i * 128:i * 128 + rows], wb[:, k, :], start=(k == 0), stop=(k == NK - 1))
                S.act(zs[0:rows, i, b * 512:(b + 1) * 512], pb[0:rows], AF.Silu)
        S.cutpoint('z')
        for b in range(2):
            wb = wblk.next()
            load_w(wb, wT_io, COL_Q + b * 256, 256, dcol=0)
            load_w(wb, wT_io, COL_Q + 512 + b * 256, 256, dcol=256)
            for i in range(9):
                rows = trows(i)
                pb = bank(psr.next())
                for k in range(NK):
                    S.mm(pb[0:rows], hT[:, k, i * 128:i * 128 + rows], wb[:, k, :], start=(k == 0), stop=(k == NK - 1))
                qf = qff.next(); qb = qbf.next()
                qv = qf[0:rows].rearrange("p (a g d) -> p g a d", a=4, g=2)
                psv = pb[0:rows].rearrange("p (g a d) -> p g a d", g=2, a=4)
                S.act(qv, psv, AF.Copy)
                rope_apply(None, qf[0:rows].rearrange("p (h d) -> p h d", h=8), rows, 8, i, tmp_r.next())
                S.copy('pool', qb[0:rows], qf[0:rows])
                pt = bankb(psr.next()).rearrange("p (a t) -> p a t", a=8)
                for a in range(4):
                    S.tr(pt[:, a, 0:rows], qb[0:rows, a * 128:(a + 1) * 128], identb[0:rows, 0:rows])
                S.copy('dve', qT[:, b * 4:b * 4 + 4, i * 128:i * 128 + rows], pt[:, 0:4, 0:rows])
        dump("xs_t", xs_t.rearrange("p a b -> p (a b)"), [128, 9 * 1024]); dump("qT", qT.rearrange("p a b -> p (a b)"), [128, 8 * TALL])
        dump("dt_o", dt_o.rearrange("p a b -> p (a b)"), [128, 144]); dump("CT", CT.rearrange("p a b -> p (a b)"), [128, 2 * TALL])
        dump("Bt", Bt.rearrange("p a b -> p (a b)"), [128, 9 * 256]); dump("zs", zs.rearrange("p a b -> p (a b)"), [128, 9 * 1024])
        dump("kT", kT, [128, 128 + TALL])
        A.release(m_inproj)

        S.cutpoint('inproj')
        A.limit = A.cap
        mixT = A.alloc([128, NK, TALL], BF16)
        att_r = Rot([A.alloc([128, 1024], BF16) for _ in range(2)])
        SCALE = 0.125

        def att_to_mixT(att, rows, tok0):
            pt = bankb(psr.next()).rearrange("p (k t) -> p k t", k=8)
            for k in range(8):
                S.tr(pt[:, k, 0:rows], att[0:rows, k * 128:(k + 1) * 128], identb[0:rows, 0:rows])
            S.act(mixT[:, 8:16, tok0:tok0 + rows], pt[:, :, 0:rows], AF.Copy)

        m_ssd = A.mark()
        rhs_cs = A.alloc([128, 16, 128], F32)
        dcy_r = Rot([A.alloc([128, 16, 128], BF16) for _ in range(2)])
        cb_sb = A.alloc([128, 256], F32)
        cbm = A.alloc([128, 2, 128], F32)
        xdt_r = Rot([A.alloc([128, 1024], BF16) for _ in range(2)])
        xw_r = Rot([A.alloc([128, 1024], BF16) for _ in range(2)])
        yA_r = Rot([A.alloc([128, 1024], F32) for _ in range(2)])
        yB_r = Rot([A.alloc([128, 1024], F32) for _ in range(1)])
        ymix_r = Rot([A.alloc([128, 1024], BF16) for _ in range(2)])
        sqj = A.alloc([128, 512], BF16)
        ex16 = A.alloc([128, 32], F32)
        g4 = A.alloc([128, 8], F32)
        m_po = A.mark()
        hbf_r = Rot([A.alloc([128, 1024], BF16) for _ in range(2)])
        Pp_r = Rot([A.alloc([128, 4, 128], BF16) for _ in range(2)])
        Po_r = Rot([A.alloc([128, 4, 128], BF16) for _ in range(2)])
        ov_r = Rot([A.alloc([128, 4, 65], F32) for _ in range(2)])
        Umf = A.alloc([128, 128], F32)
        S.ts('dve', Umf, Um, flag, None, ALU.mult)
        den4 = A.alloc([128, 16], F32)

        def ssd_chunk(c, rows, triM, UM, onesM, yoff_fn, prompt):
            tok0 = c * 128
            xs_c = xs_t[0:rows, c, :]
            xw = xw_r.next()
            cs_sb, tot_sb, dec = chunk_state(xs_c, None, dt_o[0:rows, c, :], dtA_o[0:rows, c, :], rows, False, triM, onesM, xw)
            expcs = ex16[0:rows, 0:16]
            S.act(expcs, cs_sb, AF.Exp)
            if prompt:
                hbf = hbf_r.next()
                S.copy('pool', hbf, state)
                state_update(Bt[:, c, :], xw, 128, dec, False)
            for hh in range(16):
                S.act(rhs_cs[0:rows, hh, 0:rows], triM, AF.Copy, scale=dtA_o[0:rows, c, hh:hh + 1])
            dcy = dcy_r.next()
            for q4 in range(4):
                pbq = bank(q4 % 2)[0:rows, 0:4 * rows].rearrange("p (h t) -> p h t", h=4)
                S.mm(pbq, UM, rhs_cs[0:rows, 4 * q4:4 * q4 + 4, 0:rows])
                S.act(dcy[0:rows, 4 * q4:4 * q4 + 4, 0:rows], pbq, AF.Exp)
            pcb = bank(2)
            for g in range(2):
                S.mm(pcb[0:rows, g * rows:(g + 1) * rows], BT[:, g, tok0:tok0 + rows], CT[:, g, tok0:tok0 + rows])
            S.act(cb_sb[0:rows, 0:2 * rows], pcb[0:rows, 0:2 * rows], AF.Copy)
            S.tt('dve', cbm[0:rows, :, 0:rows], cb_sb[0:rows, 0:2 * rows].rearrange("p (g t) -> p g t", g=2),
                 bc(triM.unsqueeze(1), [rows, 2, rows]), ALU.mult)
            dv = dcy[0:rows, :, 0:rows].rearrange("p (g r) t -> p g r t", g=2)
            S.tt('dve', dv, dv, bc(cbm[0:rows, :, 0:rows].unsqueeze(2), [rows, 2, 8, rows]), ALU.mult)
            xdt = xdt_r.next()
            S.tt('pool', xdt[0:rows].rearrange("p (h d) -> p h d", h=16), xs_c.rearrange("p (h d) -> p h d", h=16),
                 bc(dt_o[0:rows, c, :].unsqueeze(2), [rows, 16, 64]), ALU.mult)
            yoff_fn(bank(0), bank(1))
            for h in range(16):
                pby = bank(2 + h // 8)
                S.mm(pby[0:rows, (h % 8) * 64:(h % 8 + 1) * 64], dcy[0:rows, h, 0:rows], xdt[0:rows, h * 64:(h + 1) * 64])
            yA = yA_r.next(); yB = yB_r.next()
            for g in range(2):
                S.act(yA[0:rows, g * 512:(g + 1) * 512], bank(g)[0:rows], AF.Copy)
            S.tt('dve', yA[0:rows].rearrange("p (h d) -> p h d", h=16), yA[0:rows].rearrange("p (h d) -> p h d", h=16),
                 bc(expcs.unsqueeze(2), [rows, 16, 64]), ALU.mult)
            if c == 8: dump("yoff8", yA[0:rows], [rows, 1024])
            for g in range(2):
                S.tt('dve', yA[0:rows, g * 512:(g + 1) * 512], yA[0:rows, g * 512:(g + 1) * 512], bank(2 + g)[0:rows], ALU.add)
            if c == 8: dump("yscan8", yA[0:rows], [rows, 1024])
            S.tt('pool', yB[0:rows].rearrange("p (h d) -> p h d", h=16), xs_c.rearrange("p (h d) -> p h d", h=16),
                 bc(dskip[0:rows].unsqueeze(2), [rows, 16, 64]), ALU.mult)
            S.tt('pool', yA[0:rows], yA[0:rows], yB[0:rows], ALU.add)
            S.tt('pool', yA[0:rows], yA[0:rows], zs[0:rows, c, :], ALU.mult)
            for g in range(2):
                S.act(sqj[0:rows], yA[0:rows, g * 512:(g + 1) * 512], AF.Square, accum_out=g4[0:rows, g:g + 1])
            S.act(g4[0:rows, 2:4], g4[0:rows, 0:2], AF.Sqrt, scale=1.0 / 512, bias=eps_t[0:rows])
            S.recip(g4[0:rows, 4:6], g4[0:rows, 2:4])
            ymix = ymix_r.next()
            for g in range(2):
                S.stt('dve', ymix[0:rows, g * 512:(g + 1) * 512], yA[0:rows, g * 512:(g + 1) * 512], g4[0:rows, 4 + g:5 + g],
                      gn[0:rows, g * 512:(g + 1) * 512], ALU.mult, ALU.mult)
            if c == 8:
                dump("yg8", yA[0:rows], [rows, 1024]); dump("ymix8", ymix[0:rows], [rows, 1024]); dump("g48", g4[0:rows], [rows, 8])
            pt = bankb(psr.next()).rearrange("p (k t) -> p k t", k=8)
            for k in range(8):
                S.tr(pt[:, k, 0:rows], ymix[0:rows, k * 128:(k + 1) * 128], identb[0:rows, 0:rows])
            S.act(mixT[:, 0:8, tok0:tok0 + rows], pt[:, :, 0:rows], AF.Copy)

        def swa_block(j):
            att = att_r.next()
            kprev = kT[:, j * 128:(j + 1) * 128]; kown = kT[:, 128 + j * 128:128 + (j + 1) * 128]
            vprev = v_tok[:, 9 if j == 0 else j - 1]; vown = v_tok[:, j]
            mprev = Umf if j == 0 else Um
            for g in range(2):
                for half in range(2):
                    rhs = qT[g * 64:(g + 1) * 64, 4 * half:4 * half + 4, j * 128:(j + 1) * 128]
                    bp = bank(psr.next()); bo = bank(psr.next())
                    S.mm(bp.rearrange("p (a t) -> p a t", a=4), kprev[g * 64:(g + 1) * 64, :], rhs)
                    S.mm(bo.rearrange("p (a t) -> p a t", a=4), kown[g * 64:(g + 1) * 64, :], rhs)
                    Pp = Pp_r.next(); Po = Po_r.next()
                    S.act(Pp.rearrange("p a t -> p (a t)"), bp, AF.Exp, scale=SCALE)
                    S.act(Po.rearrange("p a t -> p (a t)"), bo, AF.Exp, scale=SCALE)
                    S.tt('pool', Pp, Pp, bc(mprev.unsqueeze(1), [128, 4, 128]), ALU.mult)
                    S.tt('pool', Po, Po, bc(tri.unsqueeze(1), [128, 4, 128]), ALU.mult)
                    bv = bank(psr.next())
                    for a4 in range(4):
                        o = bv[:, a4 * 128:a4 * 128 + 65]
                        S.mm(o, Pp[:, a4, :], vprev[:, g, :], start=True, stop=False)
                        S.mm(o, Po[:, a4, :], vown[:, g, :], start=False, stop=True)
                    ov = ov_r.next()
                    S.act(ov, bv.rearrange("p (a d) -> p a d", a=4)[:, :, 0:65], AF.Copy)
                    h0 = g * 8 + 4 * half
                    dn = den4[:, 0:4]; rd = den4[:, 4:8]
                    S.tt('dve', dn, ov[:, :, 64], esink[:, h0:h0 + 4], ALU.add)
                    S.recip(rd, dn)
                    S.tt('dve', att[:, h0 * 64:(h0 + 4) * 64].rearrange("p (a d) -> p a d", a=4), ov[:, :, 0:64],
                         bc(rd.unsqueeze(2), [128, 4, 64]), ALU.mult)
            att_to_mixT(att, 128, j * 128)

        for c in range(8):
            hb_holder = {}
            def yoff_prompt(b0, b1, c=c):
                hbf = hbf_r.items[(hbf_r.i - 1) % 2]
                for g, bk in ((0, b0), (1, b1)):
                    S.mm(bk, CT[:, g, c * 128:(c + 1) * 128], hbf[:, g * 512:(g + 1) * 512])
            psr.items = [0, 1, 2, 3]
            ssd_chunk(c, 128, tri, Um, onesf, yoff_prompt, True)
            psr.items = [4, 5, 6, 7]
            swa_block(c)
        psr.items = list(range(8))
        ost = A.alloc([128, 8, 128], F32)
        for j in range(8):
            pbo = bank(psr.next())
            S.tr(pbo[:, 0:128], state[:, j * 128:(j + 1) * 128], identf)
            S.act(ost[:, j, :], pbo[:, 0:128], AF.Copy)
        S.dma('sp', o_ssm.rearrange("(j q) n -> q j n", q=128), ost)
        A.release(m_po)
        S.cutpoint('ssd_prompt')

        CTm = A.alloc([128, 16, 2, TS], BF16)
        S.tt('dve', CTm, bc(CT[:, :, HALF:TALL].unsqueeze(1), [128, 16, 2, TS]), bc(selb.unsqueeze(2), [128, 16, 2, TS]), ALU.mult)
        xs_flat = xs_t[:, 0:8, :].rearrange("p a b -> p (a b)")
        zs_f32 = zs[:, 0:8, :].rearrange("p a b -> p (a b)").bitcast(F32)
        h0f_r = Rot([zs_f32[:, i * 1024:(i + 1) * 1024].rearrange("p (j n) -> p j n", j=8) for i in range(4)])
        h0b_r = Rot([xs_flat[:, i * 1024:(i + 1) * 1024].rearrange("p (j n) -> p j n", j=8) for i in range(2)])
        h0T_r = Rot([xs_flat[:, 2048 + i * 1024:2048 + (i + 1) * 1024] for i in range(2)])
        dtA_rep = yB_r.items[0][0:TS]
        S.copy('dve', dtA_rep.rearrange("p (h d) -> p h d", h=16), bc(dtA_o[0:TS, 8, :].unsqueeze(2), [TS, 16, 64]))
        pbd = bank(psr.next())
        for j in range(8):
            S.mm(pbd[:, j * 16:(j + 1) * 16], dtA_rep[:, j * 128:(j + 1) * 128], onehot[0:TS])
        decT = A.alloc([128, 8, 16], F32)
        S.act(decT, pbd[:, 0:128].rearrange("p (j b) -> p j b", j=8), AF.Exp)
        Bm = A.alloc([TS, 16, 256], BF16)
        S.tt('dve', Bm, bc(Bt[0:TS, 8, :].unsqueeze(1), [TS, 16, 256]), bc(onehot[0:TS].unsqueeze(2), [TS, 16, 256]), ALU.mult)
        def yoff_sample(b0, b1):
            xw_s = xw_r.items[(xw_r.i - 1) % 2]
            for b in range(NSB):
                h0f = h0f_r.next()
                S.dma('sp', h0f, sssm[b].rearrange("(j q) n -> q j n", q=128))
                h0b = h0b_r.next()
                S.act(h0b, h0f, AF.Copy)
                pt = bankb(2 + b % 2).rearrange("p (j q) -> p j q", j=8)
                for j in range(8):
                    S.tr(pt[:, j, :], h0b[:, j, :], identb)
                h0T = h0T_r.next()
                S.act(h0T, bankb(2 + b % 2), AF.Copy)
                for g, bk in ((0, b0), (1, b1)):
                    S.mm(bk[0:TS], CTm[:, b, g, :], h0T[:, g * 512:(g + 1) * 512], start=(b == 0), stop=(b == NSB - 1))
                pu = (bank(4 + 2 * (b % 2)), bank(5 + 2 * (b % 2)))
                for j in range(8):
                    S.mm(pu[j // 4][:, (j % 4) * 128:(j % 4 + 1) * 128], xw_s[0:TS, j * 128:(j + 1) * 128],
                         Bm[:, b, (j // 4) * 128:(j // 4 + 1) * 128])
                S.tt('dve', h0f, h0f, bc(decT[:, :, b:b + 1], [128, 8, 128]), ALU.mult)
                for hf in range(2):
                    hv = h0f[:, 4 * hf:4 * hf + 4, :].rearrange("p j n -> p (j n)")
                    S.tt('dve', hv, hv, pu[hf], ALU.add)
                S.dma('sp', s_ssm[b].rearrange("(j q) n -> q j n", q=128), h0f)
        ssd_chunk(8, TS, tri4[0:TS], U4[0:TS], blk4[0:TS], yoff_sample, False)
        dump("mixT", mixT.rearrange("p a b -> p (a b)"), [128, NK * TALL])
        S.cutpoint('ssd_sample')

        S.cutpoint('swa_prompt')

        m_sw = A.mark()
        kc_f = A.alloc([128, NSB, 128], F32)
        kc_b = A.alloc([128, NSB, 128], BF16)
        vc_b = A.alloc([128, NSB, 128], BF16)
        kcT = xs_flat[:, 6144:8192].rearrange("p (b j) -> p b j", b=NSB)
        S.dma('sp', kc_f, ck.rearrange("b j c -> j b c"))
        S.dma('sp', s_k[:, 0:124, :].rearrange("b j c -> j b c"), kc_f[4:128])
        S.copy('act', kc_b, kc_f)
        S.dma('sp', kc_f, cv.rearrange("b j c -> j b c"))
        S.dma('sp', s_v[:, 0:124, :].rearrange("b j c -> j b c"), kc_f[4:128])
        S.copy('act', vc_b, kc_f)
        for b8 in range(2):
            pt = bankb(psr.next()).rearrange("p (b j) -> p b j", b=8)
            for bb in range(8):
                S.tr(pt[:, bb, :], kc_b[:, b8 * 8 + bb, :], identb)
            S.act(kcT[:, b8 * 8:b8 * 8 + 8, :], pt, AF.Copy)
        S.cutpoint('sw1')
        Pc = A.alloc([128, 2, NSB, 8, 4], BF16)
        Pn = A.alloc([TS, 2, 8, TS], BF16)
        qTs = A.alloc([128, NSB, 8, 4], BF16)
        S.copy('dve', qTs, qT[:, 0:8, HALF:TALL].rearrange("p a (b l) -> p b a l", b=NSB))
        S.cutpoint('sw0a')
        bsg = (psr.next(), psr.next())
        for g in range(2):
            bk = bank(bsg[g])
            for b in range(NSB):
                S.mm(bk[:, b * 32:(b + 1) * 32], kcT[g * 64:(g + 1) * 64, b, :], qTs[g * 64:(g + 1) * 64, b].rearrange("p a l -> p (a l)"))
        S.cutpoint('sw0b')
        Pc2 = Pc.rearrange("p g b a l -> p (g b a l)")
        S.act(Pc2[:, 0:512], bank(bsg[0]), AF.Exp, scale=SCALE)
        S.act(Pc2[:, 512:1024], bank(bsg[1]), AF.Exp, scale=SCALE)
        S.cutpoint('sw1a')
        Pc3 = Pc.rearrange("p g b a l -> p (g b a) l")
        S.tt('dve', Pc3, Pc3, bc(maskc.unsqueeze(1), [128, 256, 4]), ALU.mult)
        S.cutpoint('sw1b')
        for g in range(2):
            bk = bank(psr.next())
            S.mm(bk[0:TS].rearrange("p (a t) -> p a t", a=8), kT[g * 64:(g + 1) * 64, 128 + HALF:128 + TALL], qT[g * 64:(g + 1) * 64, 0:8, HALF:TALL])
            S.act(Pn[:, g].rearrange("p a t -> p (a t)"), bk[0:TS], AF.Exp, scale=SCALE)
        S.cutpoint('sw1c')
        Pn3 = Pn.rearrange("p g a t -> p (g a) t")
        S.tt('dve', Pn3, Pn3, bc(tri4[0:TS].unsqueeze(1), [TS, 16, TS]), ALU.mult)
        S.cutpoint('sw2')
        kcf2 = kc_f.rearrange("p b c -> p (b c)")
        oc_sb = kcf2[0:64, 0:1024]; dc_sb = kcf2[0:64, 1024:2048]
        on_sb = kc_b.rearrange("p b c -> p (b c)").bitcast(F32)[0:64]
        dn_sb = vc_b.rearrange("p b c -> p (b c)").bitcast(F32)[0:64]
        bog = (psr.next(), psr.next())
        for g in range(2):
            bk = bank(bog[g])
            for b in range(NSB):
                S.mm(bk[0:64, b * 32:(b + 1) * 32], vc_b[:, b, g * 64:(g + 1) * 64], Pc[:, g, b].rearrange("p a l -> p (a l)"))
        S.act(oc_sb[:, 0:512], bank(bog[0])[0:64], AF.Copy); S.act(oc_sb[:, 512:1024], bank(bog[1])[0:64], AF.Copy)
        for hf in range(2):
            bk = bank(psr.next())
            S.mm(bk[0:64], onesb[:, 0:64], Pc2[:, hf * 512:(hf + 1) * 512])
            S.act(dc_sb[:, hf * 512:(hf + 1) * 512], bk[0:64], AF.Copy)
        vs_al = A.alloc([TS, 2, 64], BF16)
        S.copy('dve', vs_al, v_tok[0:TS, 8, :, 0:64])
        for g in range(2):
            bk = bank(psr.next())
            S.mm(bk[0:64], vs_al[:, g, :], Pn[:, g].rearrange("p a t -> p (a t)"))
            S.act(on_sb[:, g * 512:(g + 1) * 512], bk[0:64], AF.Copy)
            bk2 = bank(psr.next())
            S.mm(bk2[0:64], onesb[0:TS, 0:64], Pn[:, g].rearrange("p a t -> p (a t)"))
            S.act(dn_sb[:, g * 512:(g + 1) * 512], bk2[0:64], AF.Copy)
        S.cutpoint('sw3')
        onv = on_sb.rearrange("p (g a b l) -> p (g a) b l", g=2, a=8, b=16)
        dnv = dn_sb.rearrange("p (g a b l) -> p (g a) b l", g=2, a=8, b=16)
        for g in range(2):
            ocg = oc_sb.rearrange("p (g b a l) -> p g a b l", b=16, g=2, a=8)[:, g]
            dcg = dc_sb.rearrange("p (g b a l) -> p g a b l", b=16, g=2, a=8)[:, g]
            S.tt('dve', onv[:, g * 8:(g + 1) * 8], onv[:, g * 8:(g + 1) * 8], ocg, ALU.add)
            S.tt('dve', dnv[:, g * 8:(g + 1) * 8], dnv[:, g * 8:(g + 1) * 8], dcg, ALU.add)
        dn3 = dn_sb.rearrange("p (h t) -> p h t", h=16); on3 = on_sb.rearrange("p (h t) -> p h t", h=16)
        S.tt('dve', dn3, dn3, bc(esink[0:64].unsqueeze(2), [64, 16, TS]), ALU.add)
        S.recip(dn_sb, dn_sb)
        S.tt('dve', on_sb, on_sb, dn_sb, ALU.mult)
        S.cutpoint('sw4')
        att_s = att_r.next()
        for hh in range(2):
            bk = bank(psr.next())
            for h8 in range(8):
                S.tr(bk[0:TS, h8 * 64:(h8 + 1) * 64], on3[:, hh * 8 + h8, :], identf[0:64, 0:64])
            S.act(att_s[0:TS, hh * 512:(hh + 1) * 512], bk[0:TS], AF.Copy)
        att_to_mixT(att_s, TS, HALF)
        dump("mixT2", mixT.rearrange("p a b -> p (a b)"), [128, NK * TALL])
        S.cutpoint('swa_sample')
        A.release(m_ssd)

        R1_END = m_inproj
        HT_END = m_inproj + NK * TALL * 2
        GROUPS = ((0, 512), (512, 512), (HALF, TS))
        def r1_reset():
            A.top = m_own; A.limit = R1_END
        def r2_set(top):
            A.top = top; A.limit = A.cap
        r1_reset()
        wblk = Rot([A.alloc([128, NK, 512], BF16) for _ in range(2)])
        xin4 = Rot([A.alloc([128, 512], F32) for _ in range(4)])
        r2_set(HT_END)
        resid = A.alloc([128, 9, D], F32)
        if dbg: S.memset('pool', resid[64:128, 8, :], 0.0)
        r2_top = A.top
        w_out3 = w_out.rearrange("(k p) n -> p k n", p=128)
        for cbk in range(4):
            wb = wblk.next()
            load_w(wb, w_out3, cbk * 512, 512)
            for i in range(9):
                rows = trows(i)
                pb = bank(psr.next())
                for k in range(NK):
                    S.mm(pb[0:rows], mixT[:, k, i * 128:i * 128 + rows], wb[:, k, :], start=(k == 0), stop=(k == NK - 1))
                xt = xin4.next()
                src = xo[i * 128:(i + 1) * 128, cbk * 512:(cbk + 1) * 512] if i < 8 else xs[:, cbk * 512:(cbk + 1) * 512]
                S.dma('sp', xt[0:rows], src)
                S.tt('dve', resid[0:rows, i, cbk * 512:(cbk + 1) * 512], pb[0:rows], xt[0:rows], ALU.add)
        dump("x1", resid.rearrange("p a b -> p (a b)"), [128, 9 * D])
        S.cutpoint('outproj')

        XS = 128.0 ** -0.5
        h2T = mixT
        r1_reset()
        wblk = Rot([A.alloc([128, NK, 512], BF16) for _ in range(2)])
        mkT = A.alloc([128, 4, 256], BF16)
        mv_tok = A.alloc([128, 2, 512], BF16)
        qxT = A.alloc([128, 4, TALL], BF16)
        oxT = A.alloc([128, 4, TALL], BF16)
        m_x = A.mark()
        xin1 = A.alloc([128, D], F32)
        memT = A.alloc([128, NK, 256], BF16)
        r1_save = A.top
        r2_set(r2_top)
        xnb = Rot([A.alloc([128, D], BF16) for _ in range(1)])
        mkvf = Rot([A.alloc([128, 512], F32) for _ in range(1)])
        A.top = r1_save; A.limit = R1_END
        for mt in range(2):
            S.dma('sp', xin1, mem[mt * 128:(mt + 1) * 128, :])
            xn = xnb.next()
            norm_T(xin1, 128, nrms[:, 3, :], memT, mt * 128, xn, xn, scale_eng='dve')
        wbk = wblk.next(); load_w(wbk, w_xk.rearrange("(k p) n -> p k n", p=128), 0, 512)
        wbv = wblk.next(); load_w(wbv, w_xv.rearrange("(k p) n -> p k n", p=128), 0, 512)
        for mt in range(2):
            for (wbx, dst, is_v) in ((wbk, o_mk, False), (wbv, o_mv, True)):
                pb = bank(psr.next())
                for k in range(NK):
                    S.mm(pb, memT[:, k, mt * 128:(mt + 1) * 128], wbx[:, k, :], start=(k == 0), stop=(k == NK - 1))
                mf = mkvf.next()
                S.act(mf, pb, AF.Copy)
                S.dma('sp', dst[mt * 128:(mt + 1) * 128, :], mf)
                if is_v:
                    S.copy('pool', mv_tok[:, mt, :], mf)
        for h in range(4):
            pb = bank(psr.next())
            for k in range(NK):
                S.mm(pb[:, 0:256], wbk[:, k, h * 128:(h + 1) * 128], memT[:, k, :], start=(k == 0), stop=(k == NK - 1))
            S.act(mkT[:, h, :], pb[:, 0:256], AF.Copy)
        xnb2 = Rot([xnb.items[0], xin1.bitcast(BF16)[:, 0:D]])
        for i in range(9):
            rows = trows(i)
            xn = xnb2.next()
            norm_T(resid[0:rows, i, :], rows, nrms[:, 1, :], h2T, i * 128, xn, xn, scale_eng='dve')
        wbq = wblk.next(); load_w(wbq, w_xq.rearrange("(k p) n -> p k n", p=128), 0, 512)
        for h in range(4):
            for (g0, gsz) in GROUPS:
                pb = bank(psr.next())
                for k in range(NK):
                    S.mm(pb[:, 0:gsz], wbq[:, k, h * 128:(h + 1) * 128], h2T[:, k, g0:g0 + gsz], start=(k == 0), stop=(k == NK - 1))
                S.act(qxT[:, h, g0:g0 + gsz], pb[:, 0:gsz], AF.Copy)
        A.release(m_x)
        PT_r = Rot([A.alloc([128, 2, 512], BF16) for _ in range(2)])
        rden_r = Rot([A.alloc([128, 512], F32) for _ in range(2)])
        for h in range(4):
            for grp in range(2):
                PT = PT_r.next()
                for mt in range(2):
                    pbs = bank(psr.next())
                    S.mm(pbs, mkT[:, h, mt * 128:(mt + 1) * 128], qxT[:, h, grp * 512:(grp + 1) * 512])
                    S.act(PT[:, mt, :], pbs, AF.Exp, scale=XS)
                pbo = bank(psr.next()); pbd = bank(psr.next())
                for mt in range(2):
                    S.mm(pbo, mv_tok[:, mt, h * 128:(h + 1) * 128], PT[:, mt, :], start=(mt == 0), stop=(mt == 1))
                for mt in range(2):
                    S.mm(pbd, onesb, PT[:, mt, :], start=(mt == 0), stop=(mt == 1))
                rden = rden_r.next()
                S.recip(rden, pbd)
                S.tt('dve', oxT[:, h, grp * 512:(grp + 1) * 512], pbo, rden, ALU.mult)
        w0flat = wblk.items[0].rearrange("p k n -> p (k n)")
        mkb_r = Rot([w0flat[:, i * 1024:(i + 1) * 1024].rearrange("p (mt c) -> p mt c", mt=2) for i in range(4)])
        mvb_r = Rot([w0flat[:, (4 + i) * 1024:(5 + i) * 1024].rearrange("p (mt c) -> p mt c", mt=2) for i in range(4)])
        r1_save = A.top
        r2_set(r2_top)
        mkTb_r = Rot([A.alloc([128, 4, 2, 128], BF16) for _ in range(2)])
        PTs_r = Rot([A.alloc([128, 4, 2, 4], BF16) for _ in range(2)])
        A.top = r1_save; A.limit = R1_END
        bO = psr.next(); bD = psr.next()
        while psr.i % 8 in (bO, bD): psr.next()
        def nb2():
            n = psr.next()
            while n in (bO, bD): n = psr.next()
            return n
        for b in range(NSB):
            mkb = mkb_r.next(); mvb = mvb_r.next()
            S.dma('pool', mkb, cmk[b].rearrange("(mt m) c -> m mt c", m=128))
            S.dma('pool', mvb, cmv[b].rearrange("(mt m) c -> m mt c", m=128))
            pt = bankb(nb2()).rearrange("p (h mt m) -> p h mt m", h=4, mt=2)
            for h in range(4):
                for mt in range(2):
                    S.tr(pt[:, h, mt, :], mkb[:, mt, h * 128:(h + 1) * 128], identb)
            mkTb = mkTb_r.next()
            S.act(mkTb, pt, AF.Copy)
            pbs = bank(nb2())
            for h in range(4):
                for mt in range(2):
                    S.mm(pbs[:, (h * 2 + mt) * 4:(h * 2 + mt) * 4 + 4], mkTb[:, h, mt, :], qxT[:, h, HALF + 4 * b:HALF + 4 * b + 4])
            PTs = PTs_r.next()
            S.act(PTs.rearrange("p h mt l -> p (h mt l)"), pbs[:, 0:32], AF.Exp, scale=XS)
            for h in range(4):
                for mt in range(2):
                    S.mm(bank(bO)[:, (b * 4 + h) * 4:(b * 4 + h) * 4 + 4], mvb[:, mt, h * 128:(h + 1) * 128], PTs[:, h, mt, :], start=(mt == 0), stop=(mt == 1))
            for mt in range(2):
                S.mm(bank(bD)[:, b * 16:(b + 1) * 16].rearrange("p (h l) -> p h l", h=4), onesb, PTs[:, :, mt, :], start=(mt == 0), stop=(mt == 1))
        osb = rden_r.next(); dsb = rden_r.next()
        S.act(osb[:, 0:256], bank(bO)[:, 0:256], AF.Copy)
        S.recip(dsb[:, 0:256], bank(bD)[:, 0:256])
        S.tt('dve', osb[:, 0:256], osb[:, 0:256], dsb[:, 0:256], ALU.mult)
        S.copy('dve', oxT[:, :, HALF:TALL].rearrange("p h (b l) -> p b h l", b=16), osb[:, 0:256].rearrange("p (b h l) -> p b h l", b=16, h=4))
        wbo = wblk.next().rearrange("p k n -> p (k n)").rearrange("p (k n) -> p k n", k=4)
        w_xo3 = w_xo.rearrange("(k p) n -> p k n", p=128)
        for cbk in range(4):
            S.dma('pool', wbo[:, :, cbk * 512:(cbk + 1) * 512], w_xo3[:, :, cbk * 512:(cbk + 1) * 512])
        for cbk in range(4):
            for i in range(9):
                rows = trows(i)
                pb = bank(psr.next())
                for h in range(4):
                    S.mm(pb[0:rows], oxT[:, h, i * 128:i * 128 + rows], wbo[:, h, cbk * 512:(cbk + 1) * 512], start=(h == 0), stop=(h == 3))
                rs = resid[0:rows, i, cbk * 512:(cbk + 1) * 512]
                S.tt('dve', rs, rs, pb[0:rows], ALU.add)
        dump("x2", resid.rearrange("p a b -> p (a b)"), [128, 9 * D])
        S.cutpoint('xattn')

        h3T = mixT
        r1_reset()
        A_wgu0 = A.top
        wg_r = Rot([A.alloc([128, NK, 256], BF16) for _ in range(2)])
        wu_r = Rot([A.alloc([128, NK, 256], BF16) for _ in range(2)])
        aT = A.alloc([128, 10, TALL], BF16)
        A_wd0 = A.top
        wd_r = Rot([A.alloc([128, 10, 256], BF16) for _ in range(2)])
        wd = wd_r.items[0]
        r2_set(r2_top)
        sg_r = Rot([A.alloc([128, 512], F32) for _ in range(2)])
        xn1 = A.alloc([128, D], BF16)
        xn1_r = Rot([xn1, wd_r.items[0].rearrange("p f n -> p (f n)")[:, 0:D]])
        for i in range(9):
            rows = trows(i)
            xq = xn1_r.next()
            norm_T(resid[0:rows, i, :], rows, nrms[:, 2, :], h3T, i * 128, xq, xq, scale_eng='dve')
        wg3 = w_gate.rearrange("(k p) n -> p k n", p=128); wu3 = w_up.rearrange("(k p) n -> p k n", p=128)
        wd3 = w_down.rearrange("(f p) n -> p f n", p=128)
        f0 = 0
        parts = (10, 10, 8, 8, 8)
        wgu_flat = wg_r.items[0]
        def final_norm(i):
            rows = trows(i)
            ss = ss_r.next(); std = std_r.next(); rstd = rstd_r.next()
            S.act(xn1[0:rows], resid[0:rows, i, :], AF.Square, accum_out=ss[0:rows])
            S.act(std[0:rows], ss[0:rows], AF.Sqrt, scale=1.0 / D, bias=eps_t[0:rows])
            S.recip(rstd[0:rows], std[0:rows])
            S.stt('dve', resid[0:rows, i, :], resid[0:rows, i, :], rstd[0:rows], nfb[0:rows], ALU.mult, ALU.mult)
            S.dma('sp', y[i * 128:(i + 1) * 128, :] if i < 8 else ys[:, :], resid[0:rows, i, :])
        for pi, nf in enumerate(parts):
            last = (pi == len(parts) - 1)
            for blk in range(nf // 2):
                wg = wg_r.next(); wu = wu_r.next()
                c0 = (f0 + 2 * blk) * 128
                load_w(wg, wg3, c0, 256); load_w(wu, wu3, c0, 256)
                for c2 in range(2):
                    fl = 2 * blk + c2
                    for (g0, gsz) in GROUPS:
                        pbg = bank(psr.next()); pbu = bank(psr.next())
                        for k in range(NK):
                            S.mm(pbg[:, 0:gsz], wg[:, k, c2 * 128:(c2 + 1) * 128], h3T[:, k, g0:g0 + gsz], start=(k == 0), stop=(k == NK - 1))
                        for k in range(NK):
                            S.mm(pbu[:, 0:gsz], wu[:, k, c2 * 128:(c2 + 1) * 128], h3T[:, k, g0:g0 + gsz], start=(k == 0), stop=(k == NK - 1))
                        sg = sg_r.next()
                        S.act(sg[:, 0:gsz], pbg[:, 0:gsz], AF.Silu)
                        S.tt('dve', aT[:, fl, g0:g0 + gsz], sg[:, 0:gsz], pbu[:, 0:gsz], ALU.mult)
            if not last:
                for cbk in range(8):
                    wdb = wd_r.next()
                    for fq in range(0, nf, 5):
                        fe = min(nf, fq + 5)
                        S.dma('pool', wdb[:, fq:fe, :], wd3[:, f0 + fq:f0 + fe, cbk * 256:(cbk + 1) * 256])
                    for i in range(9):
                        rows = trows(i)
                        pb = bank(psr.next())
                        for fl in range(nf):
                            S.mm(pb[0:rows, 0:256], aT[:, fl, i * 128:i * 128 + rows], wdb[:, fl, :], start=(fl == 0), stop=(fl == nf - 1))
                        rs = resid[0:rows, i, cbk * 256:(cbk + 1) * 256]
                        S.tt('dve', rs, rs, pb[0:rows, 0:256], ALU.add)
            else:
                wdall = [A.alloc_at(A_wgu0 + cbk * 4096, [128, 8, 256], BF16) for cbk in range(8)]
                nfb = A.alloc_at(A_wd0, [128, D], F32)
                S.dma('sp', nfb, nfin.partition_broadcast(128))
                for cbk in range(8):
                    for fq in range(0, nf, 4):
                        S.dma('pool', wdall[cbk][:, fq:fq + 4, :], wd3[:, f0 + fq:f0 + fq + 4, cbk * 256:(cbk + 1) * 256])
                for i in range(9):
                    rows = trows(i)
                    for cbk in range(8):
                        pb = bank(psr.next())
                        for fl in range(nf):
                            S.mm(pb[0:rows, 0:256], aT[:, fl, i * 128:i * 128 + rows], wdall[cbk][:, fl, :], start=(fl == 0), stop=(fl == nf - 1))
                        rs = resid[0:rows, i, cbk * 256:(cbk + 1) * 256]
                        S.tt('dve', rs, rs, pb[0:rows, 0:256], ALU.add)
                    final_norm(i)
            f0 += nf
        S.cutpoint('ffn')


        stats = S.emit(); S.final_wait('sp')
    return nc, stats, A.peak

ROT_DIM = 16
ROPE_THETA = 500000.0

def make_consts():
    c = np.zeros((128, NCONST), np.float32)
    i = np.arange(128)
    c[:, C_ID:C_ID + 128] = np.eye(128)
    c[:, C_TRI:C_TRI + 128] = (i[:, None] <= i[None, :])
    c[:, C_U:C_U + 128] = (i[:, None] > i[None, :])
    c[:, C_ONES:C_ONES + 128] = 1.0
    j = np.arange(64)
    same = (j[:, None] // 4 == j[None, :] // 4)
    c[:64, C_TRI4:C_TRI4 + 64] = same & (j[:, None] <= j[None, :])
    c[:64, C_U4:C_U4 + 64] = same & (j[:, None] > j[None, :])
    c[:64, C_BLK4:C_BLK4 + 64] = same
    c[:64, C_OH:C_OH + 16] = (j[:, None] // 4 == np.arange(16)[None, :])
    c[:, C_MC:C_MC + 4] = (i[:, None] > np.arange(4)[None, :])
    selb = (np.arange(64)[None, :] // 4 == np.arange(16)[:, None])
    c[:, C_SELB:C_SELB + 1024] = selb.reshape(1, 1024)
    return c

def make_pc(core):
    own_start = (core % 2) * HALF
    pcv = np.zeros((128, NPC), np.float32)
    inv = (np.float64(ROPE_THETA) ** (-np.arange(8, dtype=np.float64) * (2.0 / ROT_DIM))).astype(np.float32)
    p = np.arange(128)
    for t in range(10):
        if t < 8: pos = own_start + t * 128 + p
        elif t == 8: pos = PAST_LEN + (p % 4)
        else: pos = own_start - 128 + p
        ang = (pos.astype(np.float32)[:, None] * inv[None, :]).astype(np.float32)
        pcv[:, t * 16:t * 16 + 8] = np.cos(ang.astype(np.float64)).astype(np.float32)
        pcv[:, t * 16 + 8:t * 16 + 16] = np.sin(ang.astype(np.float64)).astype(np.float32)
    pcv[:, 160] = float(core % 2)
    return pcv

def make_in_maps(inp, cores=range(8)):
    f = lambda a: np.ascontiguousarray(np.asarray(a, dtype=np.float32))
    consts = make_consts()
    nrm = np.stack([f(inp['norm_mix'][0]), f(inp['norm_x'][0]), f(inp['norm_ffn'][0]), f(inp['norm_mem'][0])], 0)
    nrm = np.ascontiguousarray(nrm.reshape(4, 16, 128).transpose(2, 0, 1).reshape(128, 64))
    cwv = np.concatenate([f(inp['conv_w'][0]), f(inp['conv_b'][0])[None]], 0)
    cwv = np.ascontiguousarray(cwv.reshape(5, 12, 128).transpose(2, 1, 0).reshape(128, 60))
    hv = np.stack([f(inp['dt_bias'][0]), f(inp['a_log'][0]), f(inp['d_skip'][0]), f(inp['sinks'][0])], 0)
    shared = dict(consts=consts, nrm=nrm, convw=cwv, hv=f(hv), gnorm=f(inp['gate_norm'][0]), nfin=f(inp['norm_final']),
                  w_in=f(inp['w_in'][0]), w_out=f(inp['w_out'][0]), w_xq=f(inp['w_xq'][0]), w_xk=f(inp['w_xk'][0]),
                  w_xv=f(inp['w_xv'][0]), w_xo=f(inp['w_xo'][0]), w_gate=f(inp['w_gate'][0]), w_up=f(inp['w_up'][0]),
                  w_down=f(inp['w_down'][0]))
    xp = inp['x_prompt']; maps = []
    for c in cores:
        b, h = c // 2, c % 2
        m = dict(shared)
        m['xo'] = f(xp[b, h * HALF:(h + 1) * HALF])
        m['xpv'] = f(xp[b, 0:HALF]) if h == 1 else np.zeros((HALF, D), np.float32)
        sb = slice(c * NSB, (c + 1) * NSB)
        m['xs'] = f(np.asarray(inp['x_sample'])[sb].reshape(TS, D))
        m['mem'] = f(inp['mem_prompt'][b])
        m['sssm'] = f(np.asarray(inp['state_ssm'])[0, sb].reshape(NSB, 1024, 128))
        m['sconv'] = f(np.asarray(inp['state_conv'])[0, sb].reshape(48, 1536))
        m['ck'] = f(np.asarray(inp['cache_swa_k'])[0, sb].reshape(NSB, 128, 128))
        m['cv'] = f(np.asarray(inp['cache_swa_v'])[0, sb].reshape(NSB, 128, 128))
        m['cmk'] = f(np.asarray(inp['cache_mem_k'])[0, sb].reshape(NSB, 256, 512))
        m['cmv'] = f(np.asarray(inp['cache_mem_v'])[0, sb].reshape(NSB, 256, 512))
        m['pc'] = make_pc(c)
        maps.append(m)
    return maps


def kernel(**inputs):
    from concourse.bass_utils import run_bass_kernel_spmd
    nc, stats, peak = build_nc()
    maps = make_in_maps(inputs, range(8))
    res = run_bass_kernel_spmd(nc, maps, core_ids=list(range(8)))
    R = res.results
    f32 = np.float32
    y_prompt = np.zeros((4, SEQ, D), f32); y_sample = np.zeros((128, 4, D), f32)
    p_ssm = np.zeros((1, 4, 16, 64, 128), f32); p_conv = np.zeros((1, 4, 3, 1536), f32)
    p_k = np.zeros((1, 4, 128, 2, 64), f32); p_v = np.zeros((1, 4, 128, 2, 64), f32)
    p_mk = np.zeros((1, 4, 256, 4, 128), f32); p_mv = np.zeros((1, 4, 256, 4, 128), f32)
    s_ssm = np.zeros((1, 128, 16, 64, 128), f32); s_conv = np.zeros((1, 128, 3, 1536), f32)
    s_k = np.zeros((1, 128, 128, 2, 64), f32); s_v = np.zeros((1, 128, 128, 2, 64), f32)
    for c in range(8):
        r = R[c]; b, h = c // 2, c % 2
        y_prompt[b, h * HALF:(h + 1) * HALF] = np.asarray(r['y'])
        sb = slice(c * NSB, (c + 1) * NSB)
        y_sample[sb] = np.asarray(r['ys']).reshape(NSB, 4, D)
        s_ssm[0, sb] = np.asarray(r['s_ssm']).reshape(NSB, 16, 64, 128)
        s_conv[0, sb] = np.asarray(r['s_conv']).reshape(NSB, 3, 1536)
        s_k[0, sb] = np.asarray(r['s_k']).reshape(NSB, 128, 2, 64)
        s_v[0, sb] = np.asarray(r['s_v']).reshape(NSB, 128, 2, 64)
        if h == 1:
            p_ssm[0, b] = np.asarray(r['o_ssm']).reshape(16, 64, 128)
            p_conv[0, b] = np.asarray(r['o_conv'])
            p_k[0, b] = np.asarray(r['o_k']).reshape(128, 2, 64)
            p_v[0, b] = np.asarray(r['o_v']).reshape(128, 2, 64)
        else:
            p_mk[0, b] = np.asarray(r['o_mk']).reshape(256, 4, 128)
            p_mv[0, b] = np.asarray(r['o_mv']).reshape(256, 4, 128)
    return (y_prompt, y_sample, p_ssm, p_conv, p_k, p_v, p_mk, p_mv, s_ssm, s_conv, s_k, s_v)
```

```python
import numpy as np
import concourse.bass as bass
import concourse.mybir as mybir

F32 = mybir.dt.float32; BF16 = mybir.dt.bfloat16; I32 = mybir.dt.int32
AF = mybir.ActivationFunctionType
ALU = mybir.AluOpType
AX = mybir.AxisListType

def _prod(xs):
    r = 1
    for v in xs: r *= int(v)
    return r

def region(ap):
    t = ap.tensor
    e = mybir.dt.size(ap.dtype)
    space = str(ap.space)
    dims = [(int(s), int(c)) for (s, c) in ap.ap]
    off = int(ap.offset)
    if 'DRAM' in space.upper() or 'HBM' in space.upper() or not hasattr(t, 'shape') or 'DRam' in type(t).__name__:
        lo = off; hi = off
        for s, c in dims:
            if s >= 0: hi += s * (c - 1)
            else: lo += s * (c - 1)
        return (t.name, 0, 1, lo * e, (hi + 1) * e)
    rowbytes = _prod(list(t.shape)[1:]) * mybir.dt.size(t.dtype)
    row_e = rowbytes // e
    p0 = off // row_e
    f0 = off % row_e
    s0, c0 = dims[0]
    pstep = s0 // row_e if c0 > 1 else 1
    assert c0 == 1 or s0 % row_e == 0, (ap, "first dim must be partition dim")
    p1 = p0 + (c0 - 1) * max(pstep, 0) + 1
    lo = f0; hi = f0
    for s, c in dims[1:]:
        if s >= 0: hi += s * (c - 1)
        else: lo += s * (c - 1)
    b0 = lo * e; b1 = (hi + 1) * e
    if 'PSum' in type(t).__name__:
        b0 = (b0 // 2048) * 2048; b1 = ((b1 + 2047) // 2048) * 2048
        p0 = (p0 // 32) * 32; p1 = ((p1 + 31) // 32) * 32
    return (t.name, p0, p1, b0, b1)

class _Op:
    __slots__ = ('eng', 'fn', 'is_dma', 'pos', 'deps', 'signal', 'seq', 'dsem', 'dval', 'waits', 'is_pe_w', 'odeps', 'cost', 'idx', 'nbytes', 'tag')

class Sched:
    ENGS = ('sp', 'act', 'dve', 'pool', 'pe')
    def __init__(self, nc, es, n_dma_sems=40):
        self.nc = nc
        self.eng = {'sp': nc.sync, 'act': nc.scalar, 'dve': nc.vector, 'pool': nc.gpsimd, 'pe': nc.tensor}
        self.sem = {e: es.enter_context(nc.semaphore("S_" + e)) for e in self.ENGS}
        self.dsems = [es.enter_context(nc.semaphore("D%d" % i)) for i in range(2 * n_dma_sems)]
        self.n_pool = n_dma_sems
        self.dcount = {False: 0, True: 0}
        self.dhist = {False: [], True: []}
        self.ops = []
        self.recs = {}
        self.ndma = 0
        self.dma_ops = []
        self.cnt = {e: 0 for e in self.ENGS}

    def cutpoint(self, name):
        import os
        if os.environ.get('KCUT') == name:
            self.dead = True
            print("CUT at", name)
    def op(self, eng, fn, outs, ins, dma=False):
        if getattr(self, 'dead', False):
            return None
        o = _Op(); o.eng = eng; o.fn = fn; o.is_dma = dma; o.signal = False; o.seq = 0
        o.dsem = None; o.dval = 0; o.waits = None; o.tag = None
        idx = len(self.ops)
        o.pos = self.cnt[eng]; self.cnt[eng] += 1
        deps = set()
        rregs = [region(a) for a in ins if a is not None]
        wregs = [region(a) for a in outs if a is not None]
        for (k, p0, p1, b0, b1) in rregs:
            isps = (k == 'ps')
            for r in self.recs.get(k, ()):
                if (r[5] or (isps and self.ops[r[4]].eng != eng)) and r[0] < p1 and p0 < r[1] and r[2] < b1 and b0 < r[3]:
                    deps.add(r[4])
        for (k, p0, p1, b0, b1) in wregs:
            for r in self.recs.get(k, ()):
                if r[0] < p1 and p0 < r[1] and r[2] < b1 and b0 < r[3]:
                    deps.add(r[4])
        if dma:
            n = self.n_pool
            sw = (eng == 'pool')
            cnt = self.dcount[sw]
            o.dsem = (cnt % n) + (n if sw else 0); o.dval = 16 * (cnt // n + 1)
            if cnt >= n:
                deps.add(self.dhist[sw][cnt - n])
            self.dhist[sw].append(idx); self.dcount[sw] += 1
            self.dma_ops.append(idx); self.ndma += 1
        bi = getattr(self, 'barrier_idx', 0)
        if bi:
            last = {}
            for j in range(bi):
                last[(self.ops[j].eng, self.ops[j].is_dma and j)] = j
            deps |= set(last.values())
        o.odeps = set()
        if eng == 'pe':
            o.odeps = {d for d in deps if (self.ops[d].eng == 'pe' and not self.ops[d].is_dma)}
            deps = deps - o.odeps
        o.deps = deps
        o.idx = idx
        o.cost, o.nbytes = self._cost(eng, dma, outs, ins)
        self.ops.append(o)
        for (k, p0, p1, b0, b1) in wregs:
            lst = self.recs.setdefault(k, [])
            lst[:] = [r for r in lst if not (p0 <= r[0] and r[1] <= p1 and b0 <= r[2] and r[3] <= b1)]
            lst.append([p0, p1, b0, b1, idx, True])
        for (k, p0, p1, b0, b1) in rregs:
            lst = self.recs.setdefault(k, [])
            lst.append([p0, p1, b0, b1, idx, False])
        return o

    def _cost(self, eng, dma, outs, ins):
        def fsz(a):
            n = 1
            for v in list(a.shape)[1:]: n *= int(v)
            return n
        if dma:
            nb = fsz(outs[0]) * int(outs[0].shape[0]) * max(mybir.dt.size(outs[0].dtype), mybir.dt.size(ins[0].dtype))
            return 2000.0 + nb / 250.0, nb
        if eng == 'pe':
            rhs = ins[1]
            n = max(64, fsz(rhs))
            c = n * 0.42
            if mybir.dt.size(rhs.dtype) == 4: c *= 4
            return c + 10.0, 0
        f = fsz(outs[0]) if outs else 16
        if eng == 'act': return 220.0 + f * 0.72, 0
        if eng == 'dve': return 80.0 + f * 1.05, 0
        return 150.0 + f * 2.1, 0

    def schedule(self, window=600, lat_x=900.0, lat_s=250.0):
        import heapq
        ops = self.ops; n = len(ops)
        succ = [[] for _ in range(n)]
        indeg = [0] * n
        for o in ops:
            ds = o.deps | o.odeps
            indeg[o.idx] = len(ds)
            for d in ds: succ[d].append(o.idx)
        fin = [0.0] * n
        rt = [0.0] * n
        tcur = {e: 0.0 for e in self.ENGS}
        dma_bw = [0.0]
        later = {e: [] for e in self.ENGS}
        nowq = {e: [] for e in self.ENGS}
        released = [False] * n
        waiting = set()
        base = 0; done = [False] * n; order = []
        def push(i):
            heapq.heappush(later[ops[i].eng], (rt[i], i))
        for i in range(n):
            if indeg[i] == 0:
                if i < window: push(i)
                else: waiting.add(i)
        nsched = 0
        while nsched < n:
            best = None
            for e in self.ENGS:
                L = later[e]; Q = nowq[e]
                while L and L[0][0] <= tcur[e]:
                    heapq.heappush(Q, heapq.heappop(L)[1])
                if Q:
                    cand = (tcur[e], Q[0], e, True)
                elif L:
                    cand = (L[0][0], L[0][1], e, False)
                else:
                    continue
                if best is None or cand[:2] < best[:2]: best = cand
            if best is None:
                i = min(waiting); waiting.discard(i); push(i); continue
            st, i, e, fromq = best
            if fromq: heapq.heappop(nowq[e])
            else: heapq.heappop(later[e])
            o = ops[i]
            if o.is_dma:
                issue = st
                s0 = max(issue + 600.0, dma_bw[0])
                dma_bw[0] = s0 + o.nbytes / 250.0
                fin[i] = s0 + 1400.0 + o.nbytes / 250.0
                tcur[e] = issue + 60.0
            else:
                fin[i] = st + o.cost
                tcur[e] = fin[i]
            done[i] = True; order.append(i); nsched += 1
            while base < n and done[base]: base += 1
            for j in succ[i]:
                lt = lat_s if (ops[j].eng == e and not o.is_dma) else lat_x
                if i in ops[j].odeps and i not in ops[j].deps: lt = 0.0
                if fin[i] + lt > rt[j]: rt[j] = fin[i] + lt
                indeg[j] -= 1
                if indeg[j] == 0:
                    if j < base + window: push(j)
                    else: waiting.add(j)
            if waiting:
                mv = [j for j in waiting if j < base + window]
                for j in mv:
                    waiting.discard(j); push(j)
        self.est_time = max(fin) if fin else 0.0
        return order

    def emit(self):
        import os as _o
        if _o.environ.get('KNOSCHED'):
            order = list(range(len(self.ops)))
        else:
            order = self.schedule(window=int(_o.environ.get('KWIN', '3200')))
        ops_all = self.ops
        cntp = {e: 0 for e in self.ENGS}
        for i in order:
            ops_all[i].pos = cntp[ops_all[i].eng]; cntp[ops_all[i].eng] += 1
        ops = [ops_all[i] for i in order]
        self._all = ops_all
        waited_pos = {e: {f: -1 for f in self.ENGS} for e in self.ENGS}
        waited_dma = {e: {} for e in self.ENGS}
        for o in ops:
            need_pos = {}
            need_dma = {}
            for d in o.deps:
                a = ops_all[d]
                if a.is_dma:
                    if need_dma.get(a.dsem, 0) < a.dval: need_dma[a.dsem] = a.dval
                else:
                    if need_pos.get(a.eng, -1) < a.pos: need_pos[a.eng] = a.pos
            w = []
            for f, p in need_pos.items():
                if waited_pos[o.eng][f] < p:
                    waited_pos[o.eng][f] = p
                    w.append(('e', f, p))
            for s, v in need_dma.items():
                if waited_dma[o.eng].get(s, 0) < v:
                    waited_dma[o.eng][s] = v
                    w.append(('d', s, v))
            o.waits = w
        by_eng_pos = {e: {} for e in self.ENGS}
        for o in ops:
            if not o.is_dma: by_eng_pos[o.eng][o.pos] = o
        for o in ops:
            for w in o.waits:
                if w[0] == 'e': by_eng_pos[w[1]][w[2]].signal = True
        seqc = {e: 0 for e in self.ENGS}
        for o in ops:
            if (not o.is_dma) and o.signal:
                seqc[o.eng] += 1
            o.seq = seqc[o.eng]
        nw = 0
        for o in ops:
            E = self.eng[o.eng]
            for w in o.waits:
                if w[0] == 'e':
                    E.wait_ge(self.sem[w[1]], by_eng_pos[w[1]][w[2]].seq)
                else:
                    E.wait_ge(self.dsems[w[1]], w[2])
                nw += 1
            ins = o.fn()
            if o.is_dma:
                ins.then_inc(self.dsems[o.dsem], 16)
            elif o.signal:
                ins.then_inc(self.sem[o.eng], 1)
        _sw = 0; _last = None
        for o in ops:
            if o.eng == 'act' and o.tag and any(k in o.tag for k in ('Exp', 'Ln', 'Silu', 'Sqrt')):
                grp = 'E' if ('Exp' in o.tag or 'Ln' in o.tag) else o.tag
                if grp != _last: _sw += 1; _last = grp
        self.act_switches = _sw
        self.stats = dict(n_ops=len(ops), n_waits=nw, sig={e: seqc[e] for e in self.ENGS}, cnt=dict(self.cnt), act_switches=_sw)
        return self.stats

    def final_wait(self, eng='sp'):
        E = self.eng[eng]
        last = {}
        for idx in self.dma_ops:
            o = self.ops[idx]
            last[o.dsem] = max(last.get(o.dsem, 0), o.dval)
        for s, v in last.items():
            E.wait_ge(self.dsems[s], v)

    def barrier(self):
        self.barrier_idx = len(self.ops)
        self.recs_barrier = True
    def dma(self, q, out, in_, **kw):
        E = self.eng[q]
        return self.op(q, lambda: E.dma_start(out=out, in_=in_, **kw), [out], [in_], dma=True)
    def mm(self, out, lhsT, rhs, start=True, stop=True, **kw):
        nc = self.nc
        return self.op('pe', lambda: nc.tensor.matmul(out, lhsT=lhsT, rhs=rhs, start=start, stop=stop, **kw), [out], [lhsT, rhs])
    def tr(self, out, in_, ident):
        nc = self.nc
        return self.op('pe', lambda: nc.tensor.transpose(out, in_, ident), [out], [in_, ident])
    def act(self, out, in_, func, bias=None, scale=1.0, accum_out=None, eng='act'):
        nc = self.nc
        kw = {}
        if bias is not None: kw['bias'] = bias
        if accum_out is not None: kw['accum_out'] = accum_out
        ins = [in_] + [x for x in (bias, scale) if hasattr(x, 'tensor')]
        outs = [out] + ([accum_out] if accum_out is not None else [])
        o = self.op('act', lambda: nc.scalar.activation(out=out, in_=in_, func=func, scale=scale, **kw), outs, ins)
        if o is not None: o.tag = str(func)
        return o
    def tt(self, eng, out, in0, in1, op):
        E = self.eng[eng]
        return self.op(eng, lambda: E.tensor_tensor(out=out, in0=in0, in1=in1, op=op), [out], [in0, in1])
    def ts(self, eng, out, in0, s1, s2, op0, op1=None, accum_out=None):
        E = self.eng[eng]
        kw = {}
        if op1 is not None: kw['op1'] = op1
        if accum_out is not None: kw['accum_out'] = accum_out
        ins = [in0] + [x for x in (s1, s2) if hasattr(x, 'tensor')]
        outs = [out] + ([accum_out] if accum_out is not None else [])
        return self.op(eng, lambda: E.tensor_scalar(out=out, in0=in0, scalar1=s1, scalar2=s2, op0=op0, **kw), outs, ins)
    def stt(self, eng, out, in0, scalar, in1, op0, op1, accum_out=None):
        E = self.eng[eng]
        kw = {}
        if accum_out is not None: kw['accum_out'] = accum_out
        ins = [in0, in1] + ([scalar] if hasattr(scalar, 'tensor') else [])
        outs = [out] + ([accum_out] if accum_out is not None else [])
        return self.op(eng, lambda: E.scalar_tensor_tensor(out=out, in0=in0, scalar=scalar, in1=in1, op0=op0, op1=op1, **kw), outs, ins)
    def copy(self, eng, out, in_):
        if eng == 'act':
            return self.act(out, in_, AF.Copy)
        E = self.eng[eng]
        return self.op(eng, lambda: E.tensor_copy(out=out, in_=in_), [out], [in_])
    def memset(self, eng, out, val):
        E = self.eng[eng]
        return self.op(eng, lambda: E.memset(out, val), [out], [])
    def reduce(self, eng, out, in_, op, axis=AX.X):
        E = self.eng[eng]
        return self.op(eng, lambda: E.tensor_reduce(out=out, in_=in_, axis=axis, op=op), [out], [in_])
    def recip(self, out, in_):
        nc = self.nc
        return self.op('dve', lambda: nc.vector.reciprocal(out=out, in_=in_), [out], [in_])
    def aselect(self, out, in_, pattern, compare_op, fill, base, channel_multiplier):
        nc = self.nc
        return self.op('pool', lambda: nc.gpsimd.affine_select(out=out, in_=in_, pattern=pattern, compare_op=compare_op,
                       fill=fill, base=base, channel_multiplier=channel_multiplier), [out], [in_])

import contextlib
import math

D = 2048; NK = 16
SEQ = 2048; HALF = 1024; NSB = 16; TS = 64; TALL = HALF + TS
D_STATE = 128; NH = 16; HD = 64
PAST_LEN = 16384
D_FF = 5632
EPS = 1e-5
COL_Z, COL_XBC, COL_DT, COL_Q, COL_K, COL_V = 0, 1024, 2560, 2576, 3600, 3728
IN_DIM = 3856

C_ID, C_TRI, C_U, C_ONES, C_TRI4, C_U4, C_BLK4, C_OH, C_MC, C_SELB = 0, 128, 256, 384, 512, 576, 640, 704, 720, 724
NCONST = 724 + 1024
NPC = 10 * 16 + 1

def _sz(dt):
    return mybir.dt.size(dt)

class Arena:
    def __init__(self, nc, es, nbytes):
        self.t = es.enter_context(nc.sbuf_tensor("arena", [128, nbytes // 4], F32))
        self.cap = nbytes; self.top = 0; self.peak = 0; self.limit = nbytes
    def alloc(self, shape, dtype):
        n = 1
        for s in shape[1:]: n *= s
        nb = n * _sz(dtype)
        nb4 = (nb + 31) // 32 * 32
        st = self.top
        assert st + nb4 <= self.limit, ("arena overflow", st, nb4, self.limit)
        self.top += nb4; self.peak = max(self.peak, self.top)
        ap = self.t[0:shape[0], st // 4:(st + nb4) // 4]
        if dtype != F32:
            ap = ap.bitcast(dtype)
        ap = ap[:, 0:n]
        if len(shape) == 2:
            return ap
        names = " ".join("d%d" % i for i in range(1, len(shape)))
        kw = {"d%d" % i: shape[i] for i in range(1, len(shape))}
        return ap.rearrange("p (%s) -> p %s" % (names, names), **kw)
    def alloc_at(self, offset, shape, dtype):
        save = (self.top, self.limit, self.peak)
        self.top = offset; self.limit = self.cap
        ap = self.alloc(shape, dtype)
        self.top, self.limit = save[0], save[1]
        return ap
    def mark(self): return self.top
    def release(self, m): self.top = m

def bc(ap, shape):
    return ap.to_broadcast(shape)

class Rot:
    def __init__(self, items): self.items = list(items); self.i = 0
    def next(self):
        x = self.items[self.i % len(self.items)]; self.i += 1; return x

def build_nc(dbg=()):
    nc = bass.Bass("TRN2", target_bir_lowering=False)
    def din(name, shape): return nc.dram_tensor(name, list(shape), F32, kind="ExternalInput").ap()
    def dout(name, shape): return nc.dram_tensor(name, list(shape), F32, kind="ExternalOutput").ap()
    xo = din("xo", [HALF, D]); xpv = din("xpv", [HALF, D]); xs = din("xs", [TS, D]); mem = din("mem", [256, D])
    sssm = din("sssm", [NSB, 1024, 128]); sconv = din("sconv", [48, 1536])
    ck = din("ck", [NSB, 128, 128]); cv = din("cv", [NSB, 128, 128])
    cmk = din("cmk", [NSB, 256, 512]); cmv = din("cmv", [NSB, 256, 512])
    consts = din("consts", [128, NCONST]); pc = din("pc", [128, NPC]); nrm = din("nrm", [128, 64])
    convw = din("convw", [128, 60]); hv = din("hv", [4, 16]); gnorm = din("gnorm", [1024]); nfin = din("nfin", [D])
    w_in = din("w_in", [D, IN_DIM]); w_out = din("w_out", [D, D])
    w_xq = din("w_xq", [D, 512]); w_xk = din("w_xk", [D, 512]); w_xv = din("w_xv", [D, 512]); w_xo = din("w_xo", [512, D])
    w_gate = din("w_gate", [D, D_FF]); w_up = din("w_up", [D, D_FF]); w_down = din("w_down", [D_FF, D])
    y = dout("y", [HALF, D]); ys = dout("ys", [TS, D])
    o_ssm = dout("o_ssm", [1024, 128]); o_conv = dout("o_conv", [3, 1536]); o_k = dout("o_k", [128, 128]); o_v = dout("o_v", [128, 128])
    o_mk = dout("o_mk", [256, 512]); o_mv = dout("o_mv", [256, 512])
    s_ssm = dout("s_ssm", [NSB, 1024, 128]); s_conv = dout("s_conv", [48, 1536]); s_k = dout("s_k", [NSB, 128, 128]); s_v = dout("s_v", [NSB, 128, 128])
    dbg_out = {}

    es = contextlib.ExitStack()
    with es:
        A = Arena(nc, es, 207 * 1024)
        ps_all = es.enter_context(nc.psum_tensor("ps", [128, 4096], F32))
        S = Sched(nc, es)
        def bank(i): return ps_all[:, i * 512:(i + 1) * 512]
        def bankb(i): return ps_all[:, i * 512:(i + 1) * 512].bitcast(BF16)
        def dump(name, ap, shape):
            if name in dbg:
                o = dout("dbg_" + name, shape)
                S.dma('sp' if ap.dtype == F32 else 'pool', o, ap)

        cst = A.alloc([128, NCONST], F32); S.dma('sp', cst, consts)
        pcs = A.alloc([128, NPC], F32); S.dma('sp', pcs, pc)
        nrms = A.alloc([128, 4, 16], F32); S.dma('sp', nrms, nrm.rearrange("p (a b) -> p a b", a=4))
        cw = A.alloc([128, 12, 5], F32); S.dma('sp', cw, convw.rearrange("p (a b) -> p a b", a=12))
        hvs = A.alloc([128, 4, 16], F32)
        for i in range(4):
            S.dma('sp', hvs[:, i, :], hv[i].partition_broadcast(128))
        gn = A.alloc([128, 1024], F32); S.dma('sp', gn, gnorm.partition_broadcast(128))
        identf = cst[:, C_ID:C_ID + 128]; tri = cst[:, C_TRI:C_TRI + 128]; Um = cst[:, C_U:C_U + 128]; onesf = cst[:, C_ONES:C_ONES + 128]
        tri4 = cst[:, C_TRI4:C_TRI4 + 64]; U4 = cst[:, C_U4:C_U4 + 64]; blk4 = cst[:, C_BLK4:C_BLK4 + 64]
        onehot = cst[:, C_OH:C_OH + 16]; maskc = cst[:, C_MC:C_MC + 4]
        selb = cst[:, C_SELB:C_SELB + 1024].rearrange("p (b t) -> p b t", b=16)
        one_col = cst[:, C_ONES:C_ONES + 1]
        identb = A.alloc([128, 128], BF16); S.copy('dve', identb, identf)
        onesb = A.alloc([128, 128], BF16); S.copy('dve', onesb, onesf)
        eps_t = A.alloc([128, 1], F32); S.memset('dve', eps_t, EPS)
        a_neg = A.alloc([128, 16], F32); S.act(a_neg, hvs[:, 1, :], AF.Exp); S.ts('dve', a_neg, a_neg, -1.0, None, ALU.mult)
        esink = A.alloc([128, 16], F32); S.act(esink, hvs[:, 3, :], AF.Exp)
        dtb = hvs[:, 0, :]; dskip = hvs[:, 2, :]
        flag = pcs[:, 160:161]
        def ropec(tile, rows): return pcs[0:rows, tile * 16:tile * 16 + 8]
        def ropes(tile, rows): return pcs[0:rows, tile * 16 + 8:tile * 16 + 16]
        small = A.alloc([128, 64], F32)
        ss_r = Rot([small[:, i:i + 1] for i in range(0, 4)])
        std_r = Rot([small[:, i:i + 1] for i in range(4, 8)])
        rstd_r = Rot([small[:, i:i + 1] for i in range(8, 12)])

        psr = Rot(range(8))

        state = A.alloc([128, 1024], F32)
        halo_prev = A.alloc([128, 12, 3], F32)
        kT = A.alloc([128, 128 + TALL], BF16)
        v_tok = A.alloc([128, 10, 2, 65], BF16)
        S.memset('pool', v_tok[:, :, :, 64:65], 1.0)
        wT_io = w_in.rearrange("(k p) n -> p k n", p=128)

        def load_w(dst, src3, c0, n, dcol=0):
            K = src3.shape[1]
            step = 4
            for k0 in range(0, K, step):
                k1 = min(K, k0 + step)
                S.dma('pool', dst[:, k0:k1, dcol:dcol + n], src3[:, k0:k1, c0:c0 + n])

        def norm_T(x_sb, rows, wT, dst, col0, xn, junk, scale_eng='act'):
            ss = ss_r.next(); std = std_r.next(); rstd = rstd_r.next()
            S.act(junk[0:rows], x_sb, AF.Square, accum_out=ss[0:rows])
            S.act(std[0:rows], ss[0:rows], AF.Sqrt, scale=1.0 / D, bias=eps_t[0:rows])
            S.recip(rstd[0:rows], std[0:rows])
            if scale_eng == 'act':
                S.act(xn[0:rows], x_sb, AF.Copy, scale=rstd[0:rows])
            else:
                S.ts('dve', xn[0:rows], x_sb, rstd[0:rows], None, ALU.mult)
            for half in range(2):
                pb = bankb(psr.next()).rearrange("p (k t) -> p k t", k=8)
                for k in range(8):
                    kk = half * 8 + k
                    S.tr(pb[:, k, 0:rows], xn[0:rows, kk * 128:(kk + 1) * 128], identb[0:rows, 0:rows])
                S.tt('dve', dst[:, half * 8:half * 8 + 8, col0:col0 + rows], pb[:, :, 0:rows],
                     bc(wT[:, half * 8:half * 8 + 8].unsqueeze(2), [128, 8, rows]), ALU.mult)

        def dt_proc(ps16, rows, dt_out, dtA_out, tmp):
            ta = tmp[0:rows, 0:16]; tb = tmp[0:rows, 16:32]
            S.tt('dve', ta, ps16, dtb[0:rows], ALU.add)
            S.act(tb, ta, AF.Abs)
            S.act(tb, tb, AF.Exp, scale=-1.0)
            S.act(tb, tb, AF.Ln, bias=one_col[0:rows])
            S.ts('dve', ta, ta, 0.0, None, ALU.max)
            S.tt('dve', dt_out, ta, tb, ALU.add)
            S.tt('dve', dtA_out, dt_out, a_neg[0:rows], ALU.mult)

        def rope_apply(ps3, dst3, rows, nh, tile, tmp):
            c = bc(ropec(tile, rows).unsqueeze(1), [rows, nh, 8]); s = bc(ropes(tile, rows).unsqueeze(1), [rows, nh, 8])
            def tv(o): return tmp[0:rows, o:o + nh * 8].rearrange("p (h d) -> p h d", h=nh)
            x1 = dst3[:, :, 0:8]; x2 = dst3[:, :, 8:16]
            S.tt('dve', tv(0), x1, c, ALU.mult); S.tt('dve', tv(128), x2, s, ALU.mult)
            S.tt('dve', tv(256), x2, c, ALU.mult); S.tt('dve', tv(384), x1, s, ALU.mult)
            S.tt('dve', x1, tv(0), tv(128), ALU.subtract)
            S.tt('dve', x2, tv(256), tv(384), ALU.add)

        def conv4(acc, src, cc, n, three_d=False, eng='dve'):
            def sl(k): return src[:, :, k:k + n] if three_d else src[:, k:k + n]
            S.ts(eng, acc, sl(0), cw[:, cc, 0:1], cw[:, cc, 4:5], ALU.mult, ALU.add)
            for k in range(1, 4):
                S.stt(eng, acc, sl(k), cw[:, cc, k:k + 1], acc, ALU.mult, ALU.add)

        t16 = A.alloc([128, 96], F32)
        def chunk_state(xs_t, Bt, dt_c, dtA_c, rows, first, tri_m, ones_m, xw):
            pb = bank(psr.next())
            cs_ps = pb[0:rows, 0:16]; tot_ps = pb[0:rows, 16:32]
            S.mm(cs_ps, tri_m, dtA_c)
            S.mm(tot_ps, ones_m, dtA_c)
            cs_sb = t16[0:rows, 0:16]; tot_sb = t16[0:rows, 16:32]; wend = t16[0:rows, 32:48]; dec = t16[0:rows, 48:64]
            S.act(cs_sb, cs_ps, AF.Copy)
            S.act(tot_sb, tot_ps, AF.Copy)
            S.tt('dve', wend, tot_sb, cs_sb, ALU.subtract)
            S.act(wend, wend, AF.Exp)
            S.tt('dve', wend, wend, dt_c, ALU.mult)
            S.act(dec, tot_sb, AF.Exp)
            S.tt('dve', xw[0:rows].rearrange("p (h d) -> p h d", h=16), xs_t.rearrange("p (h d) -> p h d", h=16),
                 bc(wend.unsqueeze(2), [rows, 16, 64]), ALU.mult)
            return cs_sb, tot_sb, dec

        def state_update(Bt, xw, rows, dec, first):
            for g in range(2):
                pbk = bank(psr.next())
                S.mm(pbk, Bt[:, g * 128:(g + 1) * 128], xw[0:rows, g * 512:(g + 1) * 512])
                sg = state[:, g * 512:(g + 1) * 512]
                if first:
                    S.act(sg, pbk, AF.Copy)
                else:
                    S.tt('dve', sg.rearrange("p (h d) -> p h d", h=8), sg.rearrange("p (h d) -> p h d", h=8),
                         bc(dec[:, g * 8:(g + 1) * 8].unsqueeze(2), [128, 8, 64]), ALU.mult)
                    S.tt('dve', sg, sg, pbk, ALU.add)

        m_prev = A.mark()
        hTp = A.alloc([128, NK, HALF], BF16)
        xin = Rot([A.alloc([128, D], F32) for _ in range(2)])
        xnb = Rot([A.alloc([128, D], BF16) for _ in range(2)])
        junk = A.alloc([128, D], BF16)
        wblk = Rot([A.alloc([128, NK, 512], BF16) for _ in range(2)])
        cbuf = Rot([A.alloc([128, 3 + HALF], F32) for _ in range(2)])
        cacc = Rot([A.alloc([128, HALF], F32) for _ in range(2)])
        xcb = Rot([A.alloc([128, TALL], BF16) for _ in range(2)])
        xs_p = A.alloc([128, 8, 1024], BF16)
        Bt_p = A.alloc([128, 8, 256], BF16)
        dt_p = A.alloc([128, 8, 16], F32); dtA_p = A.alloc([128, 8, 16], F32)
        tmp_r = Rot([A.alloc([128, 512], F32) for _ in range(2)])
        xw_r = Rot([A.alloc([128, 1024], BF16) for _ in range(2)])

        for i in range(8):
            xt = xin.next(); S.dma('sp', xt, xpv[i * 128:(i + 1) * 128, :])
            norm_T(xt, 128, nrms[:, 0, :], hTp, i * 128, xnb.next(), junk)
        HT_TOP = A.cap - NK * TALL * 2
        hT = A.alloc_at(HT_TOP, [128, NK, TALL], BF16)
        A.limit = HT_TOP
        for i in range(9):
            rows = 128 if i < 8 else TS
            xt = xin.next()
            S.dma('sp', xt[0:rows], xo[i * 128:(i + 1) * 128, :] if i < 8 else xs[:, :])
            xn = xnb.next()
            norm_T(xt[0:rows], rows, nrms[:, 0, :], hT, i * 128, xn, xn)
        wb = wblk.next()
        load_w(wb, wT_io, COL_DT, 16)
        load_w(wb, wT_io, COL_K, 256, dcol=16)
        for i in range(8):
            pb = bank(psr.next())
            for k in range(NK):
                S.mm(pb[:, 0:16], hTp[:, k, i * 128:(i + 1) * 128], wb[:, k, 0:16], start=(k == 0), stop=(k == NK - 1))
            dt_proc(pb[:, 0:16], 128, dt_p[:, i, :], dtA_p[:, i, :], tmp_r.next())
        def kv_tile(hT_src, c0, rows, wbk, wcol, tile, kcol, kf, vf):
            pb = bank(psr.next())
            for k in range(NK):
                S.mm(pb[0:rows, 0:256], hT_src[:, k, c0:c0 + rows], wbk[:, k, wcol:wcol + 256], start=(k == 0), stop=(k == NK - 1))
            S.act(kf[0:rows], pb[0:rows, 0:128], AF.Copy)
            rope_apply(pb[0:rows, 0:128].rearrange("p (h d) -> p h d", h=2), kf[0:rows].rearrange("p (h d) -> p h d", h=2), rows, 2, tile, tmp_r.next())
            S.act(vf[0:rows], pb[0:rows, 128:256], AF.Copy)
            kb = xnb.next()[0:rows, 0:128]
            S.copy('dve', kb, kf[0:rows])
            pt = bankb(psr.next())
            S.tr(pt[:, 0:rows], kb, identb[0:rows, 0:rows])
            S.copy('dve', kT[:, kcol:kcol + rows], pt[:, 0:rows])
            S.copy('dve', v_tok[0:rows, tile, :, 0:64], vf[0:rows].rearrange("p (g d) -> p g d", g=2))
        kvf = A.alloc([128, 2, 128], F32)
        kv_tile(hTp, 7 * 128, 128, wb, 16, 9, 0, kvf[:, 0, :], kvf[:, 1, :])
        for b in range(3):
            wb = wblk.next()
            load_w(wb, wT_io, COL_XBC + b * 512, 512)
            for j in range(4):
                cc = 4 * b + j
                cb = cbuf.next()
                S.memset('dve', cb[:, 0:3], 0.0)
                for g in range(2):
                    pb = bank(psr.next())
                    for k in range(NK):
                        S.mm(pb, wb[:, k, j * 128:(j + 1) * 128], hTp[:, k, g * 512:(g + 1) * 512], start=(k == 0), stop=(k == NK - 1))
                    S.act(cb[:, 3 + g * 512:3 + (g + 1) * 512], pb, AF.Copy)
                S.copy('dve', halo_prev[:, cc, :], cb[:, HALF:HALF + 3])
                if cc >= 10:
                    continue
                acc = cacc.next(); xc = xcb.next()
                conv4(acc, cb, cc, HALF)
                S.act(xc[:, 0:HALF], acc, AF.Silu)
                pt = bankb(psr.next()).rearrange("p (t c) -> p t c", t=8)
                for t in range(8):
                    S.tr(pt[:, t, :], xc[:, t * 128:(t + 1) * 128], identb)
                if cc < 8:
                    S.copy('dve', xs_p[:, :, cc * 128:(cc + 1) * 128], pt)
                else:
                    S.copy('dve', Bt_p[:, :, (cc - 8) * 128:(cc - 7) * 128], pt)
        for c in range(8):
            xw = xw_r.next()
            cs_sb, tot_sb, dec = chunk_state(xs_p[:, c, :], Bt_p[:, c, :], dt_p[:, c, :], dtA_p[:, c, :], 128, c == 0, tri, onesf, xw)
            state_update(Bt_p[:, c, :], xw, 128, dec, c == 0)
        S.ts('dve', state, state, flag, None, ALU.mult)
        dump("state8", state, [128, 1024]); dump("halo", halo_prev.rearrange("p a b -> p (a b)"), [128, 36])
        dump("kTp", kT[:, 0:128], [128, 128])
        A.release(m_prev)
        def trows(i): return 128 if i < 8 else TS
        import os as _os
        m_own = A.mark()
        xs_t = A.alloc([128, 9, 1024], BF16)
        Bt = A.alloc([128, 9, 256], BF16)
        BT = A.alloc([128, 2, TALL], BF16)
        CT = A.alloc([128, 2, TALL], BF16)
        dt_o = A.alloc([128, 9, 16], F32); dtA_o = A.alloc([128, 9, 16], F32)
        zs = A.alloc([128, 9, 1024], BF16)
        qT = A.alloc([128, 8, TALL], BF16)
        kvo = A.alloc([128, 4, 128], F32)
        m_inproj = A.mark()
        wblk = Rot([A.alloc([128, NK, 512], BF16) for _ in range(2)])
        tmp_r = Rot([A.alloc([128, 512], F32) for _ in range(2)])
        S.cutpoint('o1')
        cbuf = Rot([A.alloc([128, 3 + HALF], F32) for _ in range(2)])
        cacc = Rot([A.alloc([128, HALF], F32) for _ in range(2)])
        xcb = Rot([A.alloc([128, TALL], BF16) for _ in range(2)])
        pcv = Rot([A.alloc([128, 512], F32) for _ in range(2)])
        cbs = A.alloc([128, 16, 7], F32); accs = A.alloc([128, 16, 4], F32)
        scs = A.alloc([48, 1536], F32); S.dma('sp', scs, sconv)
        kbb = Rot([A.alloc([128, 128], BF16) for _ in range(2)])
        qbf = Rot([A.alloc([128, 512], BF16) for _ in range(2)])
        qff = Rot([A.alloc([128, 512], F32) for _ in range(2)])

        wb = wblk.next()
        load_w(wb, wT_io, COL_DT, 16)
        load_w(wb, wT_io, COL_K, 256, dcol=16)
        for i in range(9):
            rows = trows(i)
            pb = bank(psr.next())
            for k in range(NK):
                S.mm(pb[0:rows, 0:16], hT[:, k, i * 128:i * 128 + rows], wb[:, k, 0:16], start=(k == 0), stop=(k == NK - 1))
            dt_proc(pb[0:rows, 0:16], rows, dt_o[0:rows, i, :], dtA_o[0:rows, i, :], tmp_r.next())
        S.cutpoint('dt')
        kvcount = [0]
        def kv_tile2(i, rows, kf, vf):
            steps = _os.environ.get('KVS', '1234')
            if ',' in steps:
                steps = steps.split(',')[kvcount[0]]; kvcount[0] += 1
            pb = bank(psr.next())
            for k in range(NK):
                S.mm(pb[0:rows, 0:256], hT[:, k, i * 128:i * 128 + rows], wb[:, k, 16:272], start=(k == 0), stop=(k == NK - 1))
            S.act(kf[0:rows], pb[0:rows, 0:128], AF.Copy)
            if '2' in steps:
                rope_apply(pb[0:rows, 0:128].rearrange("p (h d) -> p h d", h=2), kf[0:rows].rearrange("p (h d) -> p h d", h=2), rows, 2, i, tmp_r.next())
            S.act(vf[0:rows], pb[0:rows, 128:256], AF.Copy)
            if '3' in steps:
                kb = kbb.next()[0:rows]
                S.copy('dve', kb, kf[0:rows])
                pt = bankb(psr.next())
                S.tr(pt[:, 0:rows], kb, identb[0:rows, 0:rows])
                S.copy('dve', kT[:, 128 + i * 128:128 + i * 128 + rows], pt[:, 0:rows])
            if '4' in steps:
                S.copy('dve', v_tok[0:rows, i, :, 0:64], vf[0:rows].rearrange("p (g d) -> p g d", g=2))
        kvtmp = Rot([A.alloc([128, 2, 128], F32) for _ in range(2)])
        if _os.environ.get('KROT'):
            for _r in _os.environ['KROT']:
                {'t': tmp_r, 'k': kbb, 'v': kvtmp, 'p': psr}[_r].next()
        for i in [int(t) for t in _os.environ.get('KVT', '0,1,2,3,4,5,6,7,8').split(',')]:
            rows = trows(i)
            if i == 7: kf, vf = kvo[:, 0, :], kvo[:, 1, :]
            elif i == 8: kf, vf = kvo[:, 2, :], kvo[:, 3, :]
            else:
                t = kvtmp.next(); kf, vf = t[:, 0, :], t[:, 1, :]
            kv_tile2(i, rows, kf, vf)
            if _os.environ.get('KBAR'): S.barrier()
        for _d in range(int(_os.environ.get('KDUM', '0'))):
            S.memset('dve', small[:, 40:41], 0.0)
        S.cutpoint('kv')
        S.dma('sp', o_k, kvo[:, 0, :]); S.dma('sp', o_v, kvo[:, 1, :])
        for l in range(4):
            S.dma('sp', s_k[:, 124 + l, :], kvo[l:TS:4, 2, :])
            S.dma('sp', s_v[:, 124 + l, :], kvo[l:TS:4, 3, :])

        S.cutpoint('kvout')
        o_conv_v = o_conv
        s_conv_v = s_conv.rearrange("(b k) c -> b k c", k=3)
        for b in range(3):
            wb = wblk.next()
            load_w(wb, wT_io, COL_XBC + b * 512, 512)
            for (i, rows) in ((7, 128), (8, TS)):
                pb = bank(psr.next())
                for k in range(NK):
                    S.mm(pb[0:rows], hT[:, k, i * 128:i * 128 + rows], wb[:, k, :], start=(k == 0), stop=(k == NK - 1))
                pv = pcv.next()
                S.act(pv[0:rows], pb[0:rows], AF.Copy)
                if i == 7:
                    S.dma('sp', o_conv_v[:, b * 512:(b + 1) * 512], pv[125:128, :])
                else:
                    for l in range(1, 4):
                        S.dma('sp', s_conv_v[:, l - 1, b * 512:(b + 1) * 512], pv[l:TS:4, :])
            for j in range(4):
                cc = 4 * b + j
                cb = cbuf.next()
                S.copy('dve', cb[:, 0:3], halo_prev[:, cc, :])
                for g in range(2):
                    pb = bank(psr.next())
                    for k in range(NK):
                        S.mm(pb, wb[:, k, j * 128:(j + 1) * 128], hT[:, k, g * 512:(g + 1) * 512], start=(k == 0), stop=(k == NK - 1))
                    S.act(cb[:, 3 + g * 512:3 + (g + 1) * 512], pb, AF.Copy)
                pb = bank(psr.next())
                for k in range(NK):
                    S.mm(pb[:, 0:TS], wb[:, k, j * 128:(j + 1) * 128], hT[:, k, HALF:TALL], start=(k == 0), stop=(k == NK - 1))
                S.act(cbs[:, :, 3:7], pb[:, 0:TS].rearrange("p (b l) -> p b l", b=16), AF.Copy)
                pbs = bank(psr.next())
                S.tr(pbs[:, 0:48], scs[:, cc * 128:(cc + 1) * 128], identf[0:48, 0:48])
                S.act(cbs[:, :, 0:3], pbs[:, 0:48].rearrange("p (b k) -> p b k", b=16), AF.Copy)
                acc = cacc.next()
                conv4(acc, cb, cc, HALF)
                conv4(accs, cbs, cc, 4, three_d=True)
                if cc < 8:
                    dst = xcb.next()
                elif cc < 10:
                    dst = BT[:, cc - 8, :]
                else:
                    dst = CT[:, cc - 10, :]
                S.act(dst[:, 0:HALF], acc, AF.Silu)
                S.act(dst[:, HALF:TALL].rearrange("p (b l) -> p b l", b=16), accs, AF.Silu)
                if cc < 10:
                    pt = bankb(psr.next()).rearrange("p (t c) -> p t c", t=8)
                    for t in range(8):
                        S.tr(pt[:, t, :], dst[:, t * 128:(t + 1) * 128], identb)
                    pt2 = bankb(psr.next())
                    S.tr(pt2[0:TS, 0:128], dst[:, HALF:TALL], identb)
                    if cc < 8:
                        S.copy('dve', xs_t[:, 0:8, cc * 128:(cc + 1) * 128], pt)
                        S.copy('dve', xs_t[0:TS, 8, cc * 128:(cc + 1) * 128], pt2[0:TS, 0:128])
                    else:
                        S.copy('dve', Bt[:, 0:8, (cc - 8) * 128:(cc - 7) * 128], pt)
                        S.copy('dve', Bt[0:TS, 8, (cc - 8) * 128:(cc - 7) * 128], pt2[0:TS, 0:128])
        S.cutpoint('xbc')
        for b in range(2):
            wb = wblk.next()
            load_w(wb, wT_io, COL_Z + b * 512, 512)
            for i in range(9):
                rows = trows(i)
                pb = bank(psr.next())
                for k in range(NK):
                    S.mm(pb[0:rows], hT[:, k, i * 128:i * 128 + rows], wb[:, k, :], start=(k == 0), stop=(k == NK - 1))
                S.act(zs[0:rows, i, b * 512:(b + 1) * 512], pb[0:rows], AF.Silu)
        S.cutpoint('z')
        for b in range(2):
            wb = wblk.next()
            load_w(wb, wT_io, COL_Q + b * 256, 256, dcol=0)
            load_w(wb, wT_io, COL_Q + 512 + b * 256, 256, dcol=256)
            for i in range(9):
                rows = trows(i)
                pb = bank(psr.next())
                for k in range(NK):
                    S.mm(pb[0:rows], hT[:, k, i * 128:i * 128 + rows], wb[:, k, :], start=(k == 0), stop=(k == NK - 1))
                qf = qff.next(); qb = qbf.next()
                qv = qf[0:rows].rearrange("p (a g d) -> p g a d", a=4, g=2)
                psv = pb[0:rows].rearrange("p (g a d) -> p g a d", g=2, a=4)
                S.act(qv, psv, AF.Copy)
                rope_apply(None, qf[0:rows].rearrange("p (h d) -> p h d", h=8), rows, 8, i, tmp_r.next())
                S.copy('pool', qb[0:rows], qf[0:rows])
                pt = bankb(psr.next()).rearrange("p (a t) -> p a t", a=8)
                for a in range(4):
                    S.tr(pt[:, a, 0:rows], qb[0:rows, a * 128:(a + 1) * 128], identb[0:rows, 0:rows])
                S.copy('dve', qT[:, b * 4:b * 4 + 4, i * 128:i * 128 + rows], pt[:, 0:4, 0:rows])
        dump("xs_t", xs_t.rearrange("p a b -> p (a b)"), [128, 9 * 1024]); dump("qT", qT.rearrange("p a b -> p (a b)"), [128, 8 * TALL])
        dump("dt_o", dt_o.rearrange("p a b -> p (a b)"), [128, 144]); dump("CT", CT.rearrange("p a b -> p (a b)"), [128, 2 * TALL])
        dump("Bt", Bt.rearrange("p a b -> p (a b)"), [128, 9 * 256]); dump("zs", zs.rearrange("p a b -> p (a b)"), [128, 9 * 1024])
        dump("kT", kT, [128, 128 + TALL])
        A.release(m_inproj)

        S.cutpoint('inproj')
        A.limit = A.cap
        mixT = A.alloc([128, NK, TALL], BF16)
        att_r = Rot([A.alloc([128, 1024], BF16) for _ in range(2)])
        SCALE = 0.125

        def att_to_mixT(att, rows, tok0):
            pt = bankb(psr.next()).rearrange("p (k t) -> p k t", k=8)
            for k in range(8):
                S.tr(pt[:, k, 0:rows], att[0:rows, k * 128:(k + 1) * 128], identb[0:rows, 0:rows])
            S.act(mixT[:, 8:16, tok0:tok0 + rows], pt[:, :, 0:rows], AF.Copy)

        m_ssd = A.mark()
        rhs_cs = A.alloc([128, 16, 128], F32)
        dcy_r = Rot([A.alloc([128, 16, 128], BF16) for _ in range(2)])
        cb_sb = A.alloc([128, 256], F32)
        cbm = A.alloc([128, 2, 128], F32)
        xdt_r = Rot([A.alloc([128, 1024], BF16) for _ in range(2)])
        xw_r = Rot([A.alloc([128, 1024], BF16) for _ in range(2)])
        yA_r = Rot([A.alloc([128, 1024], F32) for _ in range(2)])
        yB_r = Rot([A.alloc([128, 1024], F32) for _ in range(1)])
        ymix_r = Rot([A.alloc([128, 1024], BF16) for _ in range(2)])
        sqj = A.alloc([128, 512], BF16)
        ex16 = A.alloc([128, 32], F32)
        g4 = A.alloc([128, 8], F32)
        m_po = A.mark()
        hbf_r = Rot([A.alloc([128, 1024], BF16) for _ in range(2)])
        Pp_r = Rot([A.alloc([128, 4, 128], BF16) for _ in range(2)])
        Po_r = Rot([A.alloc([128, 4, 128], BF16) for _ in range(2)])
        ov_r = Rot([A.alloc([128, 4, 65], F32) for _ in range(2)])
        Umf = A.alloc([128, 128], F32)
        S.ts('dve', Umf, Um, flag, None, ALU.mult)
        den4 = A.alloc([128, 16], F32)

        def ssd_chunk(c, rows, triM, UM, onesM, yoff_fn, prompt):
            tok0 = c * 128
            xs_c = xs_t[0:rows, c, :]
            xw = xw_r.next()
            cs_sb, tot_sb, dec = chunk_state(xs_c, None, dt_o[0:rows, c, :], dtA_o[0:rows, c, :], rows, False, triM, onesM, xw)
            expcs = ex16[0:rows, 0:16]
            S.act(expcs, cs_sb, AF.Exp)
            if prompt:
                hbf = hbf_r.next()
                S.copy('act', hbf, state)
                state_update(Bt[:, c, :], xw, 128, dec, False)
            for hh in range(16):
                S.act(rhs_cs[0:rows, hh, 0:rows], triM, AF.Copy, scale=dtA_o[0:rows, c, hh:hh + 1])
            dcy = dcy_r.next()
            for q4 in range(4):
                pbq = bank(q4 % 2)[0:rows, 0:4 * rows].rearrange("p (h t) -> p h t", h=4)
                S.mm(pbq, UM, rhs_cs[0:rows, 4 * q4:4 * q4 + 4, 0:rows])
                S.act(dcy[0:rows, 4 * q4:4 * q4 + 4, 0:rows], pbq, AF.Exp)
            pcb = bank(2)
            for g in range(2):
                S.mm(pcb[0:rows, g * rows:(g + 1) * rows], BT[:, g, tok0:tok0 + rows], CT[:, g, tok0:tok0 + rows])
            S.act(cb_sb[0:rows, 0:2 * rows], pcb[0:rows, 0:2 * rows], AF.Copy)
            S.tt('dve', cbm[0:rows, :, 0:rows], cb_sb[0:rows, 0:2 * rows].rearrange("p (g t) -> p g t", g=2),
                 bc(triM.unsqueeze(1), [rows, 2, rows]), ALU.mult)
            dv = dcy[0:rows, :, 0:rows].rearrange("p (g r) t -> p g r t", g=2)
            S.tt('dve', dv, dv, bc(cbm[0:rows, :, 0:rows].unsqueeze(2), [rows, 2, 8, rows]), ALU.mult)
            xdt = xdt_r.next()
            S.tt('pool', xdt[0:rows].rearrange("p (h d) -> p h d", h=16), xs_c.rearrange("p (h d) -> p h d", h=16),
                 bc(dt_o[0:rows, c, :].unsqueeze(2), [rows, 16, 64]), ALU.mult)
            yoff_fn(bank(0), bank(1))
            for h in range(16):
                pby = bank(2 + h // 8)
                S.mm(pby[0:rows, (h % 8) * 64:(h % 8 + 1) * 64], dcy[0:rows, h, 0:rows], xdt[0:rows, h * 64:(h + 1) * 64])
            yA = yA_r.next(); yB = yB_r.next()
            for g in range(2):
                S.act(yA[0:rows, g * 512:(g + 1) * 512], bank(g)[0:rows], AF.Copy)
            S.tt('dve', yA[0:rows].rearrange("p (h d) -> p h d", h=16), yA[0:rows].rearrange("p (h d) -> p h d", h=16),
                 bc(expcs.unsqueeze(2), [rows, 16, 64]), ALU.mult)
            if c == 8: dump("yoff8", yA[0:rows], [rows, 1024])
            for g in range(2):
                S.tt('dve', yA[0:rows, g * 512:(g + 1) * 512], yA[0:rows, g * 512:(g + 1) * 512], bank(2 + g)[0:rows], ALU.add)
            if c == 8: dump("yscan8", yA[0:rows], [rows, 1024])
            S.tt('pool', yB[0:rows].rearrange("p (h d) -> p h d", h=16), xs_c.rearrange("p (h d) -> p h d", h=16),
                 bc(dskip[0:rows].unsqueeze(2), [rows, 16, 64]), ALU.mult)
            S.tt('pool', yA[0:rows], yA[0:rows], yB[0:rows], ALU.add)
            S.tt('dve', yA[0:rows], yA[0:rows], zs[0:rows, c, :], ALU.mult)
            for g in range(2):
                S.act(sqj[0:rows], yA[0:rows, g * 512:(g + 1) * 512], AF.Square, accum_out=g4[0:rows, g:g + 1])
            S.act(g4[0:rows, 2:4], g4[0:rows, 0:2], AF.Sqrt, scale=1.0 / 512, bias=eps_t[0:rows])
            S.recip(g4[0:rows, 4:6], g4[0:rows, 2:4])
            ymix = ymix_r.next()
            for g in range(2):
                S.stt('dve', ymix[0:rows, g * 512:(g + 1) * 512], yA[0:rows, g * 512:(g + 1) * 512], g4[0:rows, 4 + g:5 + g],
                      gn[0:rows, g * 512:(g + 1) * 512], ALU.mult, ALU.mult)
            if c == 8:
                dump("yg8", yA[0:rows], [rows, 1024]); dump("ymix8", ymix[0:rows], [rows, 1024]); dump("g48", g4[0:rows], [rows, 8])
            pt = bankb(psr.next()).rearrange("p (k t) -> p k t", k=8)
            for k in range(8):
                S.tr(pt[:, k, 0:rows], ymix[0:rows, k * 128:(k + 1) * 128], identb[0:rows, 0:rows])
            S.act(mixT[:, 0:8, tok0:tok0 + rows], pt[:, :, 0:rows], AF.Copy)

        def swa_block(j):
            att = att_r.next()
            kprev = kT[:, j * 128:(j + 1) * 128]; kown = kT[:, 128 + j * 128:128 + (j + 1) * 128]
            vprev = v_tok[:, 9 if j == 0 else j - 1]; vown = v_tok[:, j]
            mprev = Umf if j == 0 else Um
            for g in range(2):
                for half in range(2):
                    rhs = qT[g * 64:(g + 1) * 64, 4 * half:4 * half + 4, j * 128:(j + 1) * 128]
                    bp = bank(psr.next()); bo = bank(psr.next())
                    S.mm(bp.rearrange("p (a t) -> p a t", a=4), kprev[g * 64:(g + 1) * 64, :], rhs)
                    S.mm(bo.rearrange("p (a t) -> p a t", a=4), kown[g * 64:(g + 1) * 64, :], rhs)
                    Pp = Pp_r.next(); Po = Po_r.next()
                    S.act(Pp.rearrange("p a t -> p (a t)"), bp, AF.Exp, scale=SCALE)
                    S.act(Po.rearrange("p a t -> p (a t)"), bo, AF.Exp, scale=SCALE)
                    S.tt('pool', Pp, Pp, bc(mprev.unsqueeze(1), [128, 4, 128]), ALU.mult)
                    S.tt('pool', Po, Po, bc(tri.unsqueeze(1), [128, 4, 128]), ALU.mult)
                    bv = bank(psr.next())
                    for a4 in range(4):
                        o = bv[:, a4 * 128:a4 * 128 + 65]
                        S.mm(o, Pp[:, a4, :], vprev[:, g, :], start=True, stop=False)
                        S.mm(o, Po[:, a4, :], vown[:, g, :], start=False, stop=True)
                    ov = ov_r.next()
                    S.act(ov, bv.rearrange("p (a d) -> p a d", a=4)[:, :, 0:65], AF.Copy)
                    h0 = g * 8 + 4 * half
                    dn = den4[:, 0:4]; rd = den4[:, 4:8]
                    S.tt('dve', dn, ov[:, :, 64], esink[:, h0:h0 + 4], ALU.add)
                    S.recip(rd, dn)
                    S.tt('dve', att[:, h0 * 64:(h0 + 4) * 64].rearrange("p (a d) -> p a d", a=4), ov[:, :, 0:64],
                         bc(rd.unsqueeze(2), [128, 4, 64]), ALU.mult)
            att_to_mixT(att, 128, j * 128)

        for c in range(8):
            hb_holder = {}
            def yoff_prompt(b0, b1, c=c):
                hbf = hbf_r.items[(hbf_r.i - 1) % 2]
                for g, bk in ((0, b0), (1, b1)):
                    S.mm(bk, CT[:, g, c * 128:(c + 1) * 128], hbf[:, g * 512:(g + 1) * 512])
            psr.items = [0, 1, 2, 3]
            ssd_chunk(c, 128, tri, Um, onesf, yoff_prompt, True)
            psr.items = [4, 5, 6, 7]
            swa_block(c)
        psr.items = list(range(8))
        ost = A.alloc([128, 8, 128], F32)
        for j in range(8):
            pbo = bank(psr.next())
            S.tr(pbo[:, 0:128], state[:, j * 128:(j + 1) * 128], identf)
            S.act(ost[:, j, :], pbo[:, 0:128], AF.Copy)
        S.dma('sp', o_ssm.rearrange("(j q) n -> q j n", q=128), ost)
        A.release(m_po)
        S.cutpoint('ssd_prompt')

        CTm = A.alloc([128, 16, 2, TS], BF16)
        S.tt('dve', CTm, bc(CT[:, :, HALF:TALL].unsqueeze(1), [128, 16, 2, TS]), bc(selb.unsqueeze(2), [128, 16, 2, TS]), ALU.mult)
        xs_flat = xs_t[:, 0:8, :].rearrange("p a b -> p (a b)")
        zs_f32 = zs[:, 0:8, :].rearrange("p a b -> p (a b)").bitcast(F32)
        h0f_r = Rot([zs_f32[:, i * 1024:(i + 1) * 1024].rearrange("p (j n) -> p j n", j=8) for i in range(4)])
        h0b_r = Rot([xs_flat[:, i * 1024:(i + 1) * 1024].rearrange("p (j n) -> p j n", j=8) for i in range(2)])
        h0T_r = Rot([xs_flat[:, 2048 + i * 1024:2048 + (i + 1) * 1024] for i in range(2)])
        dtA_rep = yB_r.items[0][0:TS]
        S.copy('dve', dtA_rep.rearrange("p (h d) -> p h d", h=16), bc(dtA_o[0:TS, 8, :].unsqueeze(2), [TS, 16, 64]))
        pbd = bank(psr.next())
        for j in range(8):
            S.mm(pbd[:, j * 16:(j + 1) * 16], dtA_rep[:, j * 128:(j + 1) * 128], onehot[0:TS])
        decT = A.alloc([128, 8, 16], F32)
        S.act(decT, pbd[:, 0:128].rearrange("p (j b) -> p j b", j=8), AF.Exp)
        Bm = A.alloc([TS, 16, 256], BF16)
        S.tt('dve', Bm, bc(Bt[0:TS, 8, :].unsqueeze(1), [TS, 16, 256]), bc(onehot[0:TS].unsqueeze(2), [TS, 16, 256]), ALU.mult)
        def yoff_sample(b0, b1):
            xw_s = xw_r.items[(xw_r.i - 1) % 2]
            for b in range(NSB):
                h0f = h0f_r.next()
                S.dma('sp', h0f, sssm[b].rearrange("(j q) n -> q j n", q=128))
                h0b = h0b_r.next()
                S.act(h0b, h0f, AF.Copy)
                pt = bankb(2 + b % 2).rearrange("p (j q) -> p j q", j=8)
                for j in range(8):
                    S.tr(pt[:, j, :], h0b[:, j, :], identb)
                h0T = h0T_r.next()
                S.act(h0T, bankb(2 + b % 2), AF.Copy)
                for g, bk in ((0, b0), (1, b1)):
                    S.mm(bk[0:TS], CTm[:, b, g, :], h0T[:, g * 512:(g + 1) * 512], start=(b == 0), stop=(b == NSB - 1))
                pu = (bank(4 + 2 * (b % 2)), bank(5 + 2 * (b % 2)))
                for j in range(8):
                    S.mm(pu[j // 4][:, (j % 4) * 128:(j % 4 + 1) * 128], xw_s[0:TS, j * 128:(j + 1) * 128],
                         Bm[:, b, (j // 4) * 128:(j // 4 + 1) * 128])
                S.tt('dve', h0f, h0f, bc(decT[:, :, b:b + 1], [128, 8, 128]), ALU.mult)
                for hf in range(2):
                    hv = h0f[:, 4 * hf:4 * hf + 4, :].rearrange("p j n -> p (j n)")
                    S.tt('dve', hv, hv, pu[hf], ALU.add)
                S.dma('sp', s_ssm[b].rearrange("(j q) n -> q j n", q=128), h0f)
        ssd_chunk(8, TS, tri4[0:TS], U4[0:TS], blk4[0:TS], yoff_sample, False)
        dump("mixT", mixT.rearrange("p a b -> p (a b)"), [128, NK * TALL])
        S.cutpoint('ssd_sample')

        S.cutpoint('swa_prompt')

        m_sw = A.mark()
        kc_f = A.alloc([128, NSB, 128], F32)
        kc_b = A.alloc([128, NSB, 128], BF16)
        vc_b = A.alloc([128, NSB, 128], BF16)
        kcT = xs_flat[:, 6144:8192].rearrange("p (b j) -> p b j", b=NSB)
        S.dma('sp', kc_f, ck.rearrange("b j c -> j b c"))
        S.dma('sp', s_k[:, 0:124, :].rearrange("b j c -> j b c"), kc_f[4:128])
        S.copy('act', kc_b, kc_f)
        S.dma('sp', kc_f, cv.rearrange("b j c -> j b c"))
        S.dma('sp', s_v[:, 0:124, :].rearrange("b j c -> j b c"), kc_f[4:128])
        S.copy('act', vc_b, kc_f)
        for b8 in range(2):
            pt = bankb(psr.next()).rearrange("p (b j) -> p b j", b=8)
            for bb in range(8):
                S.tr(pt[:, bb, :], kc_b[:, b8 * 8 + bb, :], identb)
            S.act(kcT[:, b8 * 8:b8 * 8 + 8, :], pt, AF.Copy)
        S.cutpoint('sw1')
        Pc = A.alloc([128, 2, NSB, 8, 4], BF16)
        Pn = A.alloc([TS, 2, 8, TS], BF16)
        qTs = A.alloc([128, NSB, 8, 4], BF16)
        S.copy('dve', qTs, qT[:, 0:8, HALF:TALL].rearrange("p a (b l) -> p b a l", b=NSB))
        S.cutpoint('sw0a')
        bsg = (psr.next(), psr.next())
        for g in range(2):
            bk = bank(bsg[g])
            for b in range(NSB):
                S.mm(bk[:, b * 32:(b + 1) * 32], kcT[g * 64:(g + 1) * 64, b, :], qTs[g * 64:(g + 1) * 64, b].rearrange("p a l -> p (a l)"))
        S.cutpoint('sw0b')
        Pc2 = Pc.rearrange("p g b a l -> p (g b a l)")
        S.act(Pc2[:, 0:512], bank(bsg[0]), AF.Exp, scale=SCALE)
        S.act(Pc2[:, 512:1024], bank(bsg[1]), AF.Exp, scale=SCALE)
        S.cutpoint('sw1a')
        Pc3 = Pc.rearrange("p g b a l -> p (g b a) l")
        S.tt('dve', Pc3, Pc3, bc(maskc.unsqueeze(1), [128, 256, 4]), ALU.mult)
        S.cutpoint('sw1b')
        for g in range(2):
            bk = bank(psr.next())
            S.mm(bk[0:TS].rearrange("p (a t) -> p a t", a=8), kT[g * 64:(g + 1) * 64, 128 + HALF:128 + TALL], qT[g * 64:(g + 1) * 64, 0:8, HALF:TALL])
            S.act(Pn[:, g].rearrange("p a t -> p (a t)"), bk[0:TS], AF.Exp, scale=SCALE)
        S.cutpoint('sw1c')
        Pn3 = Pn.rearrange("p g a t -> p (g a) t")
        S.tt('dve', Pn3, Pn3, bc(tri4[0:TS].unsqueeze(1), [TS, 16, TS]), ALU.mult)
        S.cutpoint('sw2')
        kcf2 = kc_f.rearrange("p b c -> p (b c)")
        oc_sb = kcf2[0:64, 0:1024]; dc_sb = kcf2[0:64, 1024:2048]
        on_sb = kc_b.rearrange("p b c -> p (b c)").bitcast(F32)[0:64]
        dn_sb = vc_b.rearrange("p b c -> p (b c)").bitcast(F32)[0:64]
        bog = (psr.next(), psr.next())
        for g in range(2):
            bk = bank(bog[g])
            for b in range(NSB):
                S.mm(bk[0:64, b * 32:(b + 1) * 32], vc_b[:, b, g * 64:(g + 1) * 64], Pc[:, g, b].rearrange("p a l -> p (a l)"))
        S.act(oc_sb[:, 0:512], bank(bog[0])[0:64], AF.Copy); S.act(oc_sb[:, 512:1024], bank(bog[1])[0:64], AF.Copy)
        for hf in range(2):
            bk = bank(psr.next())
            S.mm(bk[0:64], onesb[:, 0:64], Pc2[:, hf * 512:(hf + 1) * 512])
            S.act(dc_sb[:, hf * 512:(hf + 1) * 512], bk[0:64], AF.Copy)
        vs_al = A.alloc([TS, 2, 64], BF16)
        S.copy('dve', vs_al, v_tok[0:TS, 8, :, 0:64])
        for g in range(2):
            bk = bank(psr.next())
            S.mm(bk[0:64], vs_al[:, g, :], Pn[:, g].rearrange("p a t -> p (a t)"))
            S.act(on_sb[:, g * 512:(g + 1) * 512], bk[0:64], AF.Copy)
            bk2 = bank(psr.next())
            S.mm(bk2[0:64], onesb[0:TS, 0:64], Pn[:, g].rearrange("p a t -> p (a t)"))
            S.act(dn_sb[:, g * 512:(g + 1) * 512], bk2[0:64], AF.Copy)
        S.cutpoint('sw3')
        onv = on_sb.rearrange("p (g a b l) -> p (g a) b l", g=2, a=8, b=16)
        dnv = dn_sb.rearrange("p (g a b l) -> p (g a) b l", g=2, a=8, b=16)
        for g in range(2):
            ocg = oc_sb.rearrange("p (g b a l) -> p g a b l", b=16, g=2, a=8)[:, g]
            dcg = dc_sb.rearrange("p (g b a l) -> p g a b l", b=16, g=2, a=8)[:, g]
            S.tt('dve', onv[:, g * 8:(g + 1) * 8], onv[:, g * 8:(g + 1) * 8], ocg, ALU.add)
            S.tt('dve', dnv[:, g * 8:(g + 1) * 8], dnv[:, g * 8:(g + 1) * 8], dcg, ALU.add)
        dn3 = dn_sb.rearrange("p (h t) -> p h t", h=16); on3 = on_sb.rearrange("p (h t) -> p h t", h=16)
        S.tt('dve', dn3, dn3, bc(esink[0:64].unsqueeze(2), [64, 16, TS]), ALU.add)
        S.recip(dn_sb, dn_sb)
        S.tt('dve', on_sb, on_sb, dn_sb, ALU.mult)
        S.cutpoint('sw4')
        att_s = att_r.next()
        for hh in range(2):
            bk = bank(psr.next())
            for h8 in range(8):
                S.tr(bk[0:TS, h8 * 64:(h8 + 1) * 64], on3[:, hh * 8 + h8, :], identf[0:64, 0:64])
            S.act(att_s[0:TS, hh * 512:(hh + 1) * 512], bk[0:TS], AF.Copy)
        att_to_mixT(att_s, TS, HALF)
        dump("mixT2", mixT.rearrange("p a b -> p (a b)"), [128, NK * TALL])
        S.cutpoint('swa_sample')
        A.release(m_ssd)

        R1_END = m_inproj
        HT_END = m_inproj + NK * TALL * 2
        GROUPS = ((0, 512), (512, 512), (HALF, TS))
        def r1_reset():
            A.top = m_own; A.limit = R1_END
        def r2_set(top):
            A.top = top; A.limit = A.cap
        r1_reset()
        wblk = Rot([A.alloc([128, NK, 512], BF16) for _ in range(2)])
        xin4 = Rot([A.alloc([128, 512], F32) for _ in range(4)])
        r2_set(HT_END)
        resid = A.alloc([128, 9, D], F32)
        if dbg: S.memset('pool', resid[64:128, 8, :], 0.0)
        r2_top = A.top
        w_out3 = w_out.rearrange("(k p) n -> p k n", p=128)
        for cbk in range(4):
            wb = wblk.next()
            load_w(wb, w_out3, cbk * 512, 512)
            for i in range(9):
                rows = trows(i)
                pb = bank(psr.next())
                for k in range(NK):
                    S.mm(pb[0:rows], mixT[:, k, i * 128:i * 128 + rows], wb[:, k, :], start=(k == 0), stop=(k == NK - 1))
                xt = xin4.next()
                src = xo[i * 128:(i + 1) * 128, cbk * 512:(cbk + 1) * 512] if i < 8 else xs[:, cbk * 512:(cbk + 1) * 512]
                S.dma('sp', xt[0:rows], src)
                S.tt('dve', resid[0:rows, i, cbk * 512:(cbk + 1) * 512], pb[0:rows], xt[0:rows], ALU.add)
        dump("x1", resid.rearrange("p a b -> p (a b)"), [128, 9 * D])
        S.cutpoint('outproj')

        XS = 128.0 ** -0.5
        h2T = mixT
        r1_reset()
        wblk = Rot([A.alloc([128, NK, 512], BF16) for _ in range(2)])
        mkT = A.alloc([128, 4, 256], BF16)
        mv_tok = A.alloc([128, 2, 512], BF16)
        qxT = A.alloc([128, 4, TALL], BF16)
        oxT = A.alloc([128, 4, TALL], BF16)
        m_x = A.mark()
        xin1 = A.alloc([128, D], F32)
        memT = A.alloc([128, NK, 256], BF16)
        r1_save = A.top
        r2_set(r2_top)
        xnb = Rot([A.alloc([128, D], BF16) for _ in range(1)])
        mkvf = Rot([A.alloc([128, 512], F32) for _ in range(1)])
        A.top = r1_save; A.limit = R1_END
        for mt in range(2):
            S.dma('sp', xin1, mem[mt * 128:(mt + 1) * 128, :])
            xn = xnb.next()
            norm_T(xin1, 128, nrms[:, 3, :], memT, mt * 128, xn, xn, scale_eng='dve')
        wbk = wblk.next(); load_w(wbk, w_xk.rearrange("(k p) n -> p k n", p=128), 0, 512)
        wbv = wblk.next(); load_w(wbv, w_xv.rearrange("(k p) n -> p k n", p=128), 0, 512)
        for mt in range(2):
            for (wbx, dst, is_v) in ((wbk, o_mk, False), (wbv, o_mv, True)):
                pb = bank(psr.next())
                for k in range(NK):
                    S.mm(pb, memT[:, k, mt * 128:(mt + 1) * 128], wbx[:, k, :], start=(k == 0), stop=(k == NK - 1))
                mf = mkvf.next()
                S.act(mf, pb, AF.Copy)
                S.dma('sp', dst[mt * 128:(mt + 1) * 128, :], mf)
                if is_v:
                    S.copy('pool', mv_tok[:, mt, :], mf)
        for h in range(4):
            pb = bank(psr.next())
            for k in range(NK):
                S.mm(pb[:, 0:256], wbk[:, k, h * 128:(h + 1) * 128], memT[:, k, :], start=(k == 0), stop=(k == NK - 1))
            S.act(mkT[:, h, :], pb[:, 0:256], AF.Copy)
        xnb2 = Rot([xnb.items[0], xin1.bitcast(BF16)[:, 0:D]])
        for i in range(9):
            rows = trows(i)
            xn = xnb2.next()
            norm_T(resid[0:rows, i, :], rows, nrms[:, 1, :], h2T, i * 128, xn, xn, scale_eng='dve')
        wbq = wblk.next(); load_w(wbq, w_xq.rearrange("(k p) n -> p k n", p=128), 0, 512)
        for h in range(4):
            for (g0, gsz) in GROUPS:
                pb = bank(psr.next())
                for k in range(NK):
                    S.mm(pb[:, 0:gsz], wbq[:, k, h * 128:(h + 1) * 128], h2T[:, k, g0:g0 + gsz], start=(k == 0), stop=(k == NK - 1))
                S.act(qxT[:, h, g0:g0 + gsz], pb[:, 0:gsz], AF.Copy)
        A.release(m_x)
        PT_r = Rot([A.alloc([128, 2, 512], BF16) for _ in range(2)])
        rden_r = Rot([A.alloc([128, 512], F32) for _ in range(2)])
        for h in range(4):
            for grp in range(2):
                PT = PT_r.next()
                for mt in range(2):
                    pbs = bank(psr.next())
                    S.mm(pbs, mkT[:, h, mt * 128:(mt + 1) * 128], qxT[:, h, grp * 512:(grp + 1) * 512])
                    S.act(PT[:, mt, :], pbs, AF.Exp, scale=XS)
                pbo = bank(psr.next()); pbd = bank(psr.next())
                for mt in range(2):
                    S.mm(pbo, mv_tok[:, mt, h * 128:(h + 1) * 128], PT[:, mt, :], start=(mt == 0), stop=(mt == 1))
                for mt in range(2):
                    S.mm(pbd, onesb, PT[:, mt, :], start=(mt == 0), stop=(mt == 1))
                rden = rden_r.next()
                S.recip(rden, pbd)
                S.tt('dve', oxT[:, h, grp * 512:(grp + 1) * 512], pbo, rden, ALU.mult)
        w0flat = wblk.items[0].rearrange("p k n -> p (k n)")
        mkb_r = Rot([w0flat[:, i * 1024:(i + 1) * 1024].rearrange("p (mt c) -> p mt c", mt=2) for i in range(4)])
        mvb_r = Rot([w0flat[:, (4 + i) * 1024:(5 + i) * 1024].rearrange("p (mt c) -> p mt c", mt=2) for i in range(4)])
        r1_save = A.top
        r2_set(r2_top)
        mkTb_r = Rot([A.alloc([128, 4, 2, 128], BF16) for _ in range(2)])
        PTs_r = Rot([A.alloc([128, 4, 2, 4], BF16) for _ in range(2)])
        A.top = r1_save; A.limit = R1_END
        bO = psr.next(); bD = psr.next()
        while psr.i % 8 in (bO, bD): psr.next()
        def nb2():
            n = psr.next()
            while n in (bO, bD): n = psr.next()
            return n
        for b in range(NSB):
            mkb = mkb_r.next(); mvb = mvb_r.next()
            S.dma('pool', mkb, cmk[b].rearrange("(mt m) c -> m mt c", m=128))
            S.dma('pool', mvb, cmv[b].rearrange("(mt m) c -> m mt c", m=128))
            pt = bankb(nb2()).rearrange("p (h mt m) -> p h mt m", h=4, mt=2)
            for h in range(4):
                for mt in range(2):
                    S.tr(pt[:, h, mt, :], mkb[:, mt, h * 128:(h + 1) * 128], identb)
            mkTb = mkTb_r.next()
            S.act(mkTb, pt, AF.Copy)
            pbs = bank(nb2())
            for h in range(4):
                for mt in range(2):
                    S.mm(pbs[:, (h * 2 + mt) * 4:(h * 2 + mt) * 4 + 4], mkTb[:, h, mt, :], qxT[:, h, HALF + 4 * b:HALF + 4 * b + 4])
            PTs = PTs_r.next()
            S.act(PTs.rearrange("p h mt l -> p (h mt l)"), pbs[:, 0:32], AF.Exp, scale=XS)
            for h in range(4):
                for mt in range(2):
                    S.mm(bank(bO)[:, (b * 4 + h) * 4:(b * 4 + h) * 4 + 4], mvb[:, mt, h * 128:(h + 1) * 128], PTs[:, h, mt, :], start=(mt == 0), stop=(mt == 1))
            for mt in range(2):
                S.mm(bank(bD)[:, b * 16:(b + 1) * 16].rearrange("p (h l) -> p h l", h=4), onesb, PTs[:, :, mt, :], start=(mt == 0), stop=(mt == 1))
        osb = rden_r.next(); dsb = rden_r.next()
        S.act(osb[:, 0:256], bank(bO)[:, 0:256], AF.Copy)
        S.recip(dsb[:, 0:256], bank(bD)[:, 0:256])
        S.tt('dve', osb[:, 0:256], osb[:, 0:256], dsb[:, 0:256], ALU.mult)
        S.copy('dve', oxT[:, :, HALF:TALL].rearrange("p h (b l) -> p b h l", b=16), osb[:, 0:256].rearrange("p (b h l) -> p b h l", b=16, h=4))
        wbo = wblk.next().rearrange("p k n -> p (k n)").rearrange("p (k n) -> p k n", k=4)
        w_xo3 = w_xo.rearrange("(k p) n -> p k n", p=128)
        for cbk in range(4):
            S.dma('pool', wbo[:, :, cbk * 512:(cbk + 1) * 512], w_xo3[:, :, cbk * 512:(cbk + 1) * 512])
        for cbk in range(4):
            for i in range(9):
                rows = trows(i)
                pb = bank(psr.next())
                for h in range(4):
                    S.mm(pb[0:rows], oxT[:, h, i * 128:i * 128 + rows], wbo[:, h, cbk * 512:(cbk + 1) * 512], start=(h == 0), stop=(h == 3))
                rs = resid[0:rows, i, cbk * 512:(cbk + 1) * 512]
                S.tt('dve', rs, rs, pb[0:rows], ALU.add)
        dump("x2", resid.rearrange("p a b -> p (a b)"), [128, 9 * D])
        S.cutpoint('xattn')

        h3T = mixT
        r1_reset()
        A_wgu0 = A.top
        wg_r = Rot([A.alloc([128, NK, 256], BF16) for _ in range(2)])
        wu_r = Rot([A.alloc([128, NK, 256], BF16) for _ in range(2)])
        aT = A.alloc([128, 10, TALL], BF16)
        A_wd0 = A.top
        wd_r = Rot([A.alloc([128, 10, 256], BF16) for _ in range(2)])
        wd = wd_r.items[0]
        r2_set(r2_top)
        sg_r = Rot([A.alloc([128, 512], F32) for _ in range(2)])
        xn1 = A.alloc([128, D], BF16)
        xn1_r = Rot([xn1, wd_r.items[0].rearrange("p f n -> p (f n)")[:, 0:D]])
        for i in range(9):
            rows = trows(i)
            xq = xn1_r.next()
            norm_T(resid[0:rows, i, :], rows, nrms[:, 2, :], h3T, i * 128, xq, xq, scale_eng='dve')
        wg3 = w_gate.rearrange("(k p) n -> p k n", p=128); wu3 = w_up.rearrange("(k p) n -> p k n", p=128)
        wd3 = w_down.rearrange("(f p) n -> p f n", p=128)
        f0 = 0
        parts = (10, 10, 8, 8, 8)
        wgu_flat = wg_r.items[0]
        def final_norm(i):
            rows = trows(i)
            ss = ss_r.next(); std = std_r.next(); rstd = rstd_r.next()
            S.act(xn1[0:rows], resid[0:rows, i, :], AF.Square, accum_out=ss[0:rows])
            S.act(std[0:rows], ss[0:rows], AF.Sqrt, scale=1.0 / D, bias=eps_t[0:rows])
            S.recip(rstd[0:rows], std[0:rows])
            S.stt('dve', resid[0:rows, i, :], resid[0:rows, i, :], rstd[0:rows], nfb[0:rows], ALU.mult, ALU.mult)
            S.dma('sp', y[i * 128:(i + 1) * 128, :] if i < 8 else ys[:, :], resid[0:rows, i, :])
        for pi, nf in enumerate(parts):
            last = (pi == len(parts) - 1)
            for blk in range(nf // 2):
                wg = wg_r.next(); wu = wu_r.next()
                c0 = (f0 + 2 * blk) * 128
                load_w(wg, wg3, c0, 256); load_w(wu, wu3, c0, 256)
                for c2 in range(2):
                    fl = 2 * blk + c2
                    for (g0, gsz) in GROUPS:
                        pbg = bank(psr.next()); pbu = bank(psr.next())
                        for k in range(NK):
                            S.mm(pbg[:, 0:gsz], wg[:, k, c2 * 128:(c2 + 1) * 128], h3T[:, k, g0:g0 + gsz], start=(k == 0), stop=(k == NK - 1))
                        for k in range(NK):
                            S.mm(pbu[:, 0:gsz], wu[:, k, c2 * 128:(c2 + 1) * 128], h3T[:, k, g0:g0 + gsz], start=(k == 0), stop=(k == NK - 1))
                        sg = sg_r.next()
                        S.act(sg[:, 0:gsz], pbg[:, 0:gsz], AF.Silu)
                        S.tt('dve', aT[:, fl, g0:g0 + gsz], sg[:, 0:gsz], pbu[:, 0:gsz], ALU.mult)
            if not last:
                for cbk in range(8):
                    wdb = wd_r.next()
                    for fq in range(0, nf, 5):
                        fe = min(nf, fq + 5)
                        S.dma('pool', wdb[:, fq:fe, :], wd3[:, f0 + fq:f0 + fe, cbk * 256:(cbk + 1) * 256])
                    for i in range(9):
                        rows = trows(i)
                        pb = bank(psr.next())
                        for fl in range(nf):
                            S.mm(pb[0:rows, 0:256], aT[:, fl, i * 128:i * 128 + rows], wdb[:, fl, :], start=(fl == 0), stop=(fl == nf - 1))
                        rs = resid[0:rows, i, cbk * 256:(cbk + 1) * 256]
                        S.tt('dve', rs, rs, pb[0:rows, 0:256], ALU.add)
            else:
                wdall = [A.alloc_at(A_wgu0 + cbk * 4096, [128, 8, 256], BF16) for cbk in range(8)]
                nfb = A.alloc_at(A_wd0, [128, D], F32)
                S.dma('sp', nfb, nfin.partition_broadcast(128))
                for cbk in range(8):
                    for fq in range(0, nf, 4):
                        S.dma('pool', wdall[cbk][:, fq:fq + 4, :], wd3[:, f0 + fq:f0 + fq + 4, cbk * 256:(cbk + 1) * 256])
                for i in range(9):
                    rows = trows(i)
                    for cbk in range(8):
                        pb = bank(psr.next())
                        for fl in range(nf):
                            S.mm(pb[0:rows, 0:256], aT[:, fl, i * 128:i * 128 + rows], wdall[cbk][:, fl, :], start=(fl == 0), stop=(fl == nf - 1))
                        rs = resid[0:rows, i, cbk * 256:(cbk + 1) * 256]
                        S.tt('dve', rs, rs, pb[0:rows, 0:256], ALU.add)
                    final_norm(i)
            f0 += nf
        S.cutpoint('ffn')


        stats = S.emit(); S.final_wait('sp')
    return nc, stats, A.peak

ROT_DIM = 16
ROPE_THETA = 500000.0

def make_consts():
    c = np.zeros((128, NCONST), np.float32)
    i = np.arange(128)
    c[:, C_ID:C_ID + 128] = np.eye(128)
    c[:, C_TRI:C_TRI + 128] = (i[:, None] <= i[None, :])
    c[:, C_U:C_U + 128] = (i[:, None] > i[None, :])
    c[:, C_ONES:C_ONES + 128] = 1.0
    j = np.arange(64)
    same = (j[:, None] // 4 == j[None, :] // 4)
    c[:64, C_TRI4:C_TRI4 + 64] = same & (j[:, None] <= j[None, :])
    c[:64, C_U4:C_U4 + 64] = same & (j[:, None] > j[None, :])
    c[:64, C_BLK4:C_BLK4 + 64] = same
    c[:64, C_OH:C_OH + 16] = (j[:, None] // 4 == np.arange(16)[None, :])
    c[:, C_MC:C_MC + 4] = (i[:, None] > np.arange(4)[None, :])
    selb = (np.arange(64)[None, :] // 4 == np.arange(16)[:, None])
    c[:, C_SELB:C_SELB + 1024] = selb.reshape(1, 1024)
    return c

def make_pc(core):
    own_start = (core % 2) * HALF
    pcv = np.zeros((128, NPC), np.float32)
    inv = (np.float64(ROPE_THETA) ** (-np.arange(8, dtype=np.float64) * (2.0 / ROT_DIM))).astype(np.float32)
    p = np.arange(128)
    for t in range(10):
        if t < 8: pos = own_start + t * 128 + p
        elif t == 8: pos = PAST_LEN + (p % 4)
        else: pos = own_start - 128 + p
        ang = (pos.astype(np.float32)[:, None] * inv[None, :]).astype(np.float32)
        pcv[:, t * 16:t * 16 + 8] = np.cos(ang.astype(np.float64)).astype(np.float32)
        pcv[:, t * 16 + 8:t * 16 + 16] = np.sin(ang.astype(np.float64)).astype(np.float32)
    pcv[:, 160] = float(core % 2)
    return pcv

def make_in_maps(inp, cores=range(8)):
    f = lambda a: np.ascontiguousarray(np.asarray(a, dtype=np.float32))
    consts = make_consts()
    nrm = np.stack([f(inp['norm_mix'][0]), f(inp['norm_x'][0]), f(inp['norm_ffn'][0]), f(inp['norm_mem'][0])], 0)
    nrm = np.ascontiguousarray(nrm.reshape(4, 16, 128).transpose(2, 0, 1).reshape(128, 64))
    cwv = np.concatenate([f(inp['conv_w'][0]), f(inp['conv_b'][0])[None]], 0)
    cwv = np.ascontiguousarray(cwv.reshape(5, 12, 128).transpose(2, 1, 0).reshape(128, 60))
    hv = np.stack([f(inp['dt_bias'][0]), f(inp['a_log'][0]), f(inp['d_skip'][0]), f(inp['sinks'][0])], 0)
    shared = dict(consts=consts, nrm=nrm, convw=cwv, hv=f(hv), gnorm=f(inp['gate_norm'][0]), nfin=f(inp['norm_final']),
                  w_in=f(inp['w_in'][0]), w_out=f(inp['w_out'][0]), w_xq=f(inp['w_xq'][0]), w_xk=f(inp['w_xk'][0]),
                  w_xv=f(inp['w_xv'][0]), w_xo=f(inp['w_xo'][0]), w_gate=f(inp['w_gate'][0]), w_up=f(inp['w_up'][0]),
                  w_down=f(inp['w_down'][0]))
    xp = inp['x_prompt']; maps = []
    for c in cores:
        b, h = c // 2, c % 2
        m = dict(shared)
        m['xo'] = f(xp[b, h * HALF:(h + 1) * HALF])
        m['xpv'] = f(xp[b, 0:HALF]) if h == 1 else np.zeros((HALF, D), np.float32)
        sb = slice(c * NSB, (c + 1) * NSB)
        m['xs'] = f(np.asarray(inp['x_sample'])[sb].reshape(TS, D))
        m['mem'] = f(inp['mem_prompt'][b])
        m['sssm'] = f(np.asarray(inp['state_ssm'])[0, sb].reshape(NSB, 1024, 128))
        m['sconv'] = f(np.asarray(inp['state_conv'])[0, sb].reshape(48, 1536))
        m['ck'] = f(np.asarray(inp['cache_swa_k'])[0, sb].reshape(NSB, 128, 128))
        m['cv'] = f(np.asarray(inp['cache_swa_v'])[0, sb].reshape(NSB, 128, 128))
        m['cmk'] = f(np.asarray(inp['cache_mem_k'])[0, sb].reshape(NSB, 256, 512))
        m['cmv'] = f(np.asarray(inp['cache_mem_v'])[0, sb].reshape(NSB, 256, 512))
        m['pc'] = make_pc(c)
        maps.append(m)
    return maps


def kernel(**inputs):
    from concourse.bass_utils import run_bass_kernel_spmd
    nc, stats, peak = build_nc()
    maps = make_in_maps(inputs, range(8))
    res = run_bass_kernel_spmd(nc, maps, core_ids=list(range(8)))
    R = res.results
    f32 = np.float32
    y_prompt = np.zeros((4, SEQ, D), f32); y_sample = np.zeros((128, 4, D), f32)
    p_ssm = np.zeros((1, 4, 16, 64, 128), f32); p_conv = np.zeros((1, 4, 3, 1536), f32)
    p_k = np.zeros((1, 4, 128, 2, 64), f32); p_v = np.zeros((1, 4, 128, 2, 64), f32)
    p_mk = np.zeros((1, 4, 256, 4, 128), f32); p_mv = np.zeros((1, 4, 256, 4, 128), f32)
    s_ssm = np.zeros((1, 128, 16, 64, 128), f32); s_conv = np.zeros((1, 128, 3, 1536), f32)
    s_k = np.zeros((1, 128, 128, 2, 64), f32); s_v = np.zeros((1, 128, 128, 2, 64), f32)
    for c in range(8):
        r = R[c]; b, h = c // 2, c % 2
        y_prompt[b, h * HALF:(h + 1) * HALF] = np.asarray(r['y'])
        sb = slice(c * NSB, (c + 1) * NSB)
        y_sample[sb] = np.asarray(r['ys']).reshape(NSB, 4, D)
        s_ssm[0, sb] = np.asarray(r['s_ssm']).reshape(NSB, 16, 64, 128)
        s_conv[0, sb] = np.asarray(r['s_conv']).reshape(NSB, 3, 1536)
        s_k[0, sb] = np.asarray(r['s_k']).reshape(NSB, 128, 2, 64)
        s_v[0, sb] = np.asarray(r['s_v']).reshape(NSB, 128, 2, 64)
        if h == 1:
            p_ssm[0, b] = np.asarray(r['o_ssm']).reshape(16, 64, 128)
            p_conv[0, b] = np.asarray(r['o_conv'])
            p_k[0, b] = np.asarray(r['o_k']).reshape(128, 2, 64)
            p_v[0, b] = np.asarray(r['o_v']).reshape(128, 2, 64)
        else:
            p_mk[0, b] = np.asarray(r['o_mk']).reshape(256, 4, 128)
            p_mv[0, b] = np.asarray(r['o_mv']).reshape(256, 4, 128)
    return (y_prompt, y_sample, p_ssm, p_conv, p_k, p_v, p_mk, p_mv, s_ssm, s_conv, s_k, s_v)
```
